# Optimizing a Trainium2 kernel written in Bass

```python
import jax, jax.numpy as jnp
from jax import lax
import numpy as np

D_MODEL = 1024
BATCH = 8
SEQ = 2048
DEPTH = 2
DEC_BATCH = 32
DEC_SEQ = 4
PAST_LEN = 8192
PAGE_SIZE = 128

N_HEADS = 8
HEAD_DIM = 64
D_ATT = N_HEADS * HEAD_DIM
D_CONV = D_MODEL - D_ATT
CONV_W = 31
POOL_WINDOWS = (2, 4, 8, 16)
POOL_GROUP = D_MODEL // len(POOL_WINDOWS)
POOL_BUF = max(POOL_WINDOWS) - 1
D_FF = 2816
Q_BLOCK = 128
D_IN = 3 * D_ATT + N_HEADS + 2 * D_CONV
N_EVEN = (DEPTH + 1) // 2
N_ODD = DEPTH // 2
RMS_EPS = 1e-6
LN_EPS = 1e-5
NEG_INF = -1e30
FGATE_BIAS_INIT = 7.0

kernel_name = "hybrid_conv_fox_pool_macaron_step"


def rmsnorm(x, g):
    xf = x.astype(jnp.float32)
    y = xf * lax.rsqrt(jnp.mean(xf * xf, axis=-1, keepdims=True) + RMS_EPS)
    return (y * g.astype(jnp.float32)).astype(x.dtype)


def swiglu(h, wg, wu, wd):
    return (jax.nn.silu(h @ wg) * (h @ wu)) @ wd


def fox_attend(q, cq, qpos, k, ck, kpos, v):
    s = jnp.einsum('bqhd,bkhd->bhqk', q, k).astype(jnp.float32) * (HEAD_DIM ** -0.5)
    bias = jnp.swapaxes(cq, 1, 2)[..., :, None] - jnp.swapaxes(ck, 1, 2)[..., None, :]
    mask = kpos[None, :] <= qpos[:, None]
    p = jax.nn.softmax(jnp.where(mask, s + bias, NEG_INF), axis=-1)
    return jnp.einsum('bhqk,bkhd->bqhd', p.astype(v.dtype), v)


def fox_prompt(q, k, v, logf):
    B, S, H, Dh = q.shape
    c = jnp.cumsum(logf, axis=1)
    nb = S // Q_BLOCK
    qb = jnp.swapaxes(q.reshape(B, nb, Q_BLOCK, H, Dh), 0, 1)
    cb = jnp.swapaxes(c.reshape(B, nb, Q_BLOCK, H), 0, 1)
    kpos = jnp.arange(S)

    def one_block(args):
        qi, ci, i = args
        qpos = i * Q_BLOCK + jnp.arange(Q_BLOCK)
        return fox_attend(qi, ci, qpos, k, c, kpos, v)

    o = lax.map(one_block, (qb, cb, jnp.arange(nb)))
    return jnp.swapaxes(o, 0, 1).reshape(B, S, H * Dh)


def fox_sample(q, k, v, logf, k_past, v_past, logf_past):
    B, T, H, Dh = q.shape
    P = k_past.shape[1]
    cn = jnp.cumsum(logf, axis=1)
    suffix = lax.cumsum(logf_past, axis=1, reverse=True) - logf_past
    ck = jnp.concatenate([-suffix, cn], axis=1)
    k_all = jnp.concatenate([k_past, k], axis=1)
    v_all = jnp.concatenate([v_past, v], axis=1)
    qpos = P + jnp.arange(T)
    kpos = jnp.arange(P + T)
    o = fox_attend(q, cn, qpos, k_all, ck, kpos, v_all)
    return o.reshape(B, T, H * Dh)


def even_project(h, w_in, b_f):
    B, T, _ = h.shape
    z = h @ w_in
    q, k, v, fg, a, g = jnp.split(
        z, [D_ATT, 2 * D_ATT, 3 * D_ATT, 3 * D_ATT + N_HEADS, 3 * D_ATT + N_HEADS + D_CONV], axis=-1)
    heads = lambda t: t.reshape(B, T, N_HEADS, HEAD_DIM)
    logf = jax.nn.log_sigmoid((fg + b_f).astype(jnp.float32))
    u = a * jax.nn.sigmoid(g)
    return heads(q), heads(k), heads(v), logf, u


def conv_tail(buf, dw_w, dw_b, ln_g, ln_b):
    y = lax.conv_general_dilated(buf, dw_w[:, None, :].astype(buf.dtype), (1,), 'VALID',
                                 dimension_numbers=('NWC', 'WIO', 'NWC'),
                                 feature_group_count=D_CONV) + dw_b
    yf = y.astype(jnp.float32)
    mu = jnp.mean(yf, axis=-1, keepdims=True)
    var = jnp.mean((yf - mu) ** 2, axis=-1, keepdims=True)
    yn = (yf - mu) * lax.rsqrt(var + LN_EPS) * ln_g.astype(jnp.float32) + ln_b.astype(jnp.float32)
    return jax.nn.silu(yn).astype(buf.dtype)


def multiscale_pool_mix(buf, pos0, w_groups, scale):
    B, L, D = buf.shape
    T = L - POOL_BUF
    bf = buf.astype(jnp.float32)
    cs = jnp.concatenate([jnp.zeros((B, 1, D), jnp.float32), jnp.cumsum(bf, axis=1)], axis=1)
    hi = cs[:, POOL_BUF + 1:]
    x_cur = bf[:, POOL_BUF:]
    pos = pos0 + jnp.arange(T)
    outs = []
    for gi, w in enumerate(POOL_WINDOWS):
        sl = slice(gi * POOL_GROUP, (gi + 1) * POOL_GROUP)
        lo = cs[:, POOL_BUF + 1 - w: POOL_BUF + 1 - w + T, sl]
        cnt = jnp.minimum(pos + 1, w).astype(jnp.float32)[None, :, None]
        mixed = (hi[..., sl] - lo) / cnt - x_cur[..., sl]
        outs.append(mixed.astype(buf.dtype) @ w_groups[gi])
    return jnp.concatenate(outs, axis=-1) * scale


def setup_inputs(seed: int = 0) -> dict:
    key = jax.random.key(seed)
    ks = jax.random.split(key, 24)
    f32 = jnp.float32
    n_pages = PAST_LEN // PAGE_SIZE
    n_used = DEC_BATCH * n_pages
    n_pool = n_used + n_used // 4
    nrm = lambda k, shp, s=1.0: jax.random.normal(k, shp, f32) * s
    page_table = jax.random.permutation(ks[0], n_pool)[:n_used].reshape(DEC_BATCH, n_pages).astype(jnp.int32)
    return {
        "x_prompt": nrm(ks[1], (BATCH, SEQ, D_MODEL)),
        "x_sample": nrm(ks[2], (DEC_BATCH, DEC_SEQ, D_MODEL)),
        "cache_k": nrm(ks[3], (N_EVEN, n_pool, PAGE_SIZE, N_HEADS, HEAD_DIM)),
        "cache_v": nrm(ks[4], (N_EVEN, n_pool, PAGE_SIZE, N_HEADS, HEAD_DIM)),
        "cache_logf": jax.nn.log_sigmoid(nrm(ks[5], (N_EVEN, n_pool, PAGE_SIZE, N_HEADS)) + FGATE_BIAS_INIT),
        "state_conv": nrm(ks[6], (N_EVEN, DEC_BATCH, CONV_W - 1, D_CONV), 0.5),
        "state_pool": nrm(ks[7], (N_ODD, DEC_BATCH, POOL_BUF, D_MODEL)),
        "page_table": page_table,
        "norm_g": 1.0 + nrm(ks[8], (DEPTH, 6, D_MODEL), 0.05),
        "ffn_w_gate": nrm(ks[9], (DEPTH, 2, D_MODEL, D_FF), D_MODEL ** -0.5),
        "ffn_w_up": nrm(ks[10], (DEPTH, 2, D_MODEL, D_FF), D_MODEL ** -0.5),
        "ffn_w_down": nrm(ks[11], (DEPTH, 2, D_FF, D_MODEL), D_FF ** -0.5),
        "mix_w_in": nrm(ks[12], (N_EVEN, D_MODEL, D_IN), D_MODEL ** -0.5),
        "fgate_b": FGATE_BIAS_INIT + nrm(ks[13], (N_EVEN, N_HEADS), 0.1),
        "conv_dw_w": nrm(ks[14], (N_EVEN, CONV_W, D_CONV), CONV_W ** -0.5),
        "conv_dw_b": nrm(ks[15], (N_EVEN, D_CONV), 0.02),
        "conv_ln_g": 1.0 + nrm(ks[16], (N_EVEN, D_CONV), 0.05),
        "conv_ln_b": nrm(ks[17], (N_EVEN, D_CONV), 0.02),
        "mix_w_out": nrm(ks[18], (N_EVEN, D_ATT + D_CONV, D_MODEL), (D_ATT + D_CONV) ** -0.5),
        "pool_w": nrm(ks[19], (N_ODD, len(POOL_WINDOWS), POOL_GROUP, POOL_GROUP), POOL_GROUP ** -0.5),
        "pool_scale": 1.0 + nrm(ks[20], (N_ODD, D_MODEL), 0.1),
    }


def reference(x_prompt, x_sample, cache_k, cache_v, cache_logf, state_conv, state_pool, page_table,
              norm_g, ffn_w_gate, ffn_w_up, ffn_w_down, mix_w_in, fgate_b, conv_dw_w, conv_dw_b,
              conv_ln_g, conv_ln_b, mix_w_out, pool_w, pool_scale):
    B = x_prompt.shape[0]
    Bd = x_sample.shape[0]
    P = page_table.shape[1] * cache_k.shape[2]
    xp, xs = x_prompt, x_sample
    kp_l, vp_l, lp_l, cp_l, pp_l = [], [], [], [], []
    ks_l, vs_l, ls_l, cs_l, ps_l = [], [], [], [], []
    for layer in range(DEPTH):
        g = norm_g[layer]
        w1 = (ffn_w_gate[layer, 0], ffn_w_up[layer, 0], ffn_w_down[layer, 0])
        xp = xp + 0.5 * rmsnorm(swiglu(rmsnorm(xp, g[0]), *w1), g[1])
        xs = xs + 0.5 * rmsnorm(swiglu(rmsnorm(xs, g[0]), *w1), g[1])
        if layer % 2 == 0:
            e = layer // 2
            conv_p = (conv_dw_w[e], conv_dw_b[e], conv_ln_g[e], conv_ln_b[e])
            hp = rmsnorm(xp, g[2])
            q, k, v, lf, u = even_project(hp, mix_w_in[e], fgate_b[e])
            att = fox_prompt(q, k, v, lf)
            cbuf = jnp.concatenate([jnp.zeros((B, CONV_W - 1, D_CONV), u.dtype), u], axis=1)
            cv = conv_tail(cbuf, *conv_p)
            mp = jnp.concatenate([att, cv], axis=-1) @ mix_w_out[e]
            xp = xp + rmsnorm(mp, g[3])
            kp_l.append(k); vp_l.append(v); lp_l.append(lf); cp_l.append(u[:, -(CONV_W - 1):])
            hs = rmsnorm(xs, g[2])
            q, k, v, lf, u = even_project(hs, mix_w_in[e], fgate_b[e])
            k_past = cache_k[e][page_table].reshape(Bd, P, N_HEADS, HEAD_DIM)
            v_past = cache_v[e][page_table].reshape(Bd, P, N_HEADS, HEAD_DIM)
            lf_past = cache_logf[e][page_table].reshape(Bd, P, N_HEADS).astype(jnp.float32)
            att = fox_sample(q, k.astype(k_past.dtype), v.astype(v_past.dtype), lf, k_past, v_past, lf_past)
            cbuf = jnp.concatenate([state_conv[e].astype(u.dtype), u], axis=1)
            cv = conv_tail(cbuf, *conv_p)
            ms = jnp.concatenate([att.astype(cv.dtype), cv], axis=-1) @ mix_w_out[e]
            xs = xs + rmsnorm(ms, g[3])
            ks_l.append(k); vs_l.append(v); ls_l.append(lf); cs_l.append(cbuf[:, -(CONV_W - 1):])
        else:
            o = layer // 2
            hp = rmsnorm(xp, g[2])
            pbuf = jnp.concatenate([jnp.zeros((B, POOL_BUF, D_MODEL), hp.dtype), hp], axis=1)
            mp = multiscale_pool_mix(pbuf, 0, pool_w[o], pool_scale[o])
            xp = xp + rmsnorm(mp, g[3])
            pp_l.append(hp[:, -POOL_BUF:])
            hs = rmsnorm(xs, g[2])
            sbuf = jnp.concatenate([state_pool[o].astype(hs.dtype), hs], axis=1)
            ms = multiscale_pool_mix(sbuf, P, pool_w[o], pool_scale[o])
            xs = xs + rmsnorm(ms, g[3])
            ps_l.append(sbuf[:, -POOL_BUF:])
        w2 = (ffn_w_gate[layer, 1], ffn_w_up[layer, 1], ffn_w_down[layer, 1])
        xp = xp + 0.5 * rmsnorm(swiglu(rmsnorm(xp, g[4]), *w2), g[5])
        xs = xs + 0.5 * rmsnorm(swiglu(rmsnorm(xs, g[4]), *w2), g[5])
    return (xp, xs,
            jnp.stack(kp_l), jnp.stack(vp_l), jnp.stack(lp_l), jnp.stack(cp_l), jnp.stack(pp_l),
            jnp.stack(ks_l), jnp.stack(vs_l), jnp.stack(ls_l), jnp.stack(cs_l), jnp.stack(ps_l))
```

```python
import os
import bisect
from contextlib import ExitStack
import numpy as np
import concourse.bass as bass
import concourse.mybir as mybir
from concourse.bass_utils import run_bass_kernel_spmd

F32 = mybir.dt.float32
BF16 = mybir.dt.bfloat16
I32 = mybir.dt.int32
U32 = mybir.dt.uint32
AF = mybir.ActivationFunctionType
ALU = mybir.AluOpType

NCORES = 8
D = 1024
DC = 8
DFF = 2816
FC = 22
SEQ = 2048
NSAMP = 16
NTOK = SEQ + NSAMP
TILES = [(0, 512), (512, 512), (1024, 512), (1536, 512), (2048, 16)]
MACROS = [[0, 1], [2, 3, 4]]
NH = 8
HD = 64
DATT = 512
DCONV = 512
CONVW = 31
DIN = 2568
PAST = 8192
PAGE = 128
NPAGES = 64
NPOOLPG = 2560
RMS_EPS = 1e-6
LN_EPS = 1e-5
NEG = -30000.0


class Tk:
    __slots__ = ("eng", "seq")

    def __init__(self, eng, seq):
        self.eng = eng
        self.seq = seq


class Res:
    __slots__ = ("name", "w", "r", "dsem", "dval", "excl")

    def __init__(self, name, excl=False):
        self.name = name
        self.excl = excl
        self.w = None
        self.r = {}
        self.dsem = None
        self.dval = 0


class Eng:
    def __init__(self, name, handle, sem):
        self.name = name
        self.h = handle
        self.sem = sem
        self.count = 0
        self.seq = 0
        self.last = None
        self.last_inced = True
        self.inc_seqs = []
        self.inc_tickets = []
        self.known = {}


class KB:
    def __init__(self):
        self.nc = bass.Bass("TRN2", target_bir_lowering=False)
        nc = self.nc
        self.E = {}
        for nm, h in (("pe", nc.tensor), ("act", nc.scalar), ("dve", nc.vector),
                      ("pool", nc.gpsimd), ("sp", nc.sync)):
            self.E[nm] = Eng(nm, h, nc.alloc_semaphore("sem_" + nm))
        self.pending_dma = {}
        self.nres = 0

    def res(self, name):
        return Res(name)

    def _resolve(self, tk):
        e = self.E[tk.eng]
        i = bisect.bisect_left(e.inc_seqs, tk.seq)
        if i < len(e.inc_seqs):
            return e.sem, e.inc_tickets[i]
        assert e.last is not None and not e.last_inced
        e.last.then_inc(e.sem, 1)
        e.count += 1
        e.last_inced = True
        e.inc_seqs.append(e.seq)
        e.inc_tickets.append(e.count)
        return e.sem, e.count

    def _wait(self, E, sem, val):
        e = self.E[E]
        k = id(sem)
        if e.known.get(k, 0) >= val:
            return
        e.h.wait_ge(sem, val)
        e.known[k] = val

    def _need(self, E, dep):
        if dep is None:
            return
        if isinstance(dep, Tk):
            if dep.eng == E and E in ("pe", "sp"):
                return
            sem, val = self._resolve(dep)
            self._wait(E, sem, val)
        else:
            _, sem, val = dep
            self._wait(E, sem, val)

    def _deps(self, E, reads, writes):
        for r in reads:
            self._need(E, r.w)
        for w in writes:
            self._need(E, w.w)
            for v in w.r.values():
                self._need(E, v)

    def op(self, E, fn, reads=(), writes=()):
        e = self.E[E]
        if any(r.excl for r in reads):
            writes = list(writes) + [r for r in reads if r.excl]
            reads = [r for r in reads if not r.excl]
        self._deps(E, reads, writes)
        ins = fn(e.h)
        e.seq += 1
        e.last = ins
        e.last_inced = False
        tk = Tk(E, e.seq)
        for r in reads:
            r.r[E] = tk
        for w in writes:
            w.w = tk
            w.r = {}
        return ins

    def dma(self, Q, out, in_, reads=(), writes=(), owner=None, fn=None):
        e = self.E[Q]
        if e.seq and not e.last_inced:
            self._resolve(Tk(Q, e.seq))
        self._deps(Q, reads, writes)
        own = owner or (writes[0] if writes else reads[0])
        if own.dsem is None:
            self.nres += 1
            own.dsem = self.nc.alloc_semaphore("d%d_%s" % (self.nres, own.name))
        if own.dval:
            self._wait(Q, own.dsem, own.dval)
        ins = fn(e.h) if fn is not None else e.h.dma_start(out=out, in_=in_)
        own.dval += 16
        ins.then_inc(own.dsem, 16)
        e.seq += 1
        e.last = ins
        e.last_inced = True
        dep = ("dma", own.dsem, own.dval)
        for r in reads:
            r.r[("dma", id(own.dsem))] = dep
        for w in writes:
            w.w = dep
            w.r = {}
        self.pending_dma[id(own.dsem)] = (own.dsem, own.dval)
        return ins

    def barrier(self, engines=("pe", "act", "dve", "pool", "sp")):
        tks = {}
        for nm in ("pe", "act", "dve", "pool"):
            e = self.E[nm]
            if e.seq:
                if not e.last_inced:
                    tks[nm] = self._resolve(Tk(nm, e.seq))
                elif e.inc_seqs:
                    tks[nm] = (e.sem, e.inc_tickets[-1])
        for F in engines:
            for nm, (sem, val) in tks.items():
                if nm != F:
                    self._wait(F, sem, val)
            for sem, val in self.pending_dma.values():
                self._wait(F, sem, val)
        self.pending_dma = {}

    def finish(self):
        self.barrier(engines=("sp",))


def ts(t):
    o, n = TILES[t]
    return slice(o, o + n)


def build(stage, NPOOLPG=NPOOLPG):
    kb = KB()
    nc = kb.nc

    def din(name, shape, dt=F32):
        return nc.dram_tensor(name, list(shape), dt, kind="ExternalInput").ap()

    def dout(name, shape, dt=F32):
        return nc.dram_tensor(name, list(shape), dt, kind="ExternalOutput").ap()

    x_prompt = din("x_prompt", [SEQ, D])
    x_sample = din("x_sample", [NSAMP, D])
    cache_k = din("cache_k", [NPOOLPG * PAGE, DATT])
    cache_v = din("cache_v", [NPOOLPG * PAGE, DATT])
    cache_logf = din("cache_logf", [NPOOLPG * PAGE, NH])
    state_conv = din("state_conv", [4, 30, DCONV])
    state_pool = din("state_pool", [4, 15, D])
    page_table = din("page_table", [4, NPAGES], I32)
    norm_g = din("norm_g", [12 * DC, 128])
    ffn_wg = din("ffn_w_gate", [4, D, DFF])
    ffn_wu = din("ffn_w_up", [4, D, DFF])
    ffn_wd = din("ffn_w_down", [4, DFF, D])
    mix_w_in = din("mix_w_in", [D, DIN])
    fgate_b = din("fgate_b", [NH])
    conv_dw_w = din("conv_dw_w", [CONVW * 4, 128])
    conv_dw_b = din("conv_dw_b", [4, 128])
    conv_ln_g = din("conv_ln_g", [4, 128])
    conv_ln_b = din("conv_ln_b", [4, 128])
    mix_w_out = din("mix_w_out", [D, D])
    pool_w = din("pool_w", [4, 256, 256])
    pool_scale = din("pool_scale", [DC, 128])

    y_prompt = dout("y_prompt", [SEQ, D])
    y_sample = dout("y_sample", [NSAMP, D])
    o_kp = dout("new_k_prompt", [SEQ, DATT])
    o_vp = dout("new_v_prompt", [SEQ, DATT])
    o_lp = dout("new_logf_prompt", [SEQ, NH])
    o_cp = dout("new_conv_prompt", [30, DCONV])
    o_pp = dout("new_pool_prompt", [15, D])
    o_ks = dout("new_k_sample", [NSAMP, DATT])
    o_vs = dout("new_v_sample", [NSAMP, DATT])
    o_ls = dout("new_logf_sample", [NSAMP, NH])
    o_cs = dout("new_conv_sample", [4, 30, DCONV])
    o_ps = dout("new_pool_sample", [4, 15, D])

    uniq = [0]

    def sb(name, shape, dt=F32, es=None):
        if es is None:
            return nc.alloc_sbuf_tensor(name, list(shape), dt).ap()
        uniq[0] += 1
        return es.enter_context(nc.sbuf_tensor("%s_u%d" % (name, uniq[0]), list(shape), dt)).ap()

    X = sb("X", [128, DC, NTOK])
    RX = [kb.res("X%d" % t) for t in range(5)]
    identf = sb("identf", [128, 128])
    identb = sb("identb", [128, 128], BF16)
    onesb = sb("onesb", [128, 128], BF16)
    ones1 = sb("ones1", [128, 128], BF16)
    C1 = sb("C1", [128, 116])
    C2 = sb("C2", [128, 124])
    ghalf = sb("ghalf", [128, 4 * DC])
    onesf = sb("onesf", [128, 128])
    epsr = sb("epsr", [128, 1])
    epsl = sb("epsl", [128, 1])
    RC = kb.res("consts")
    NWGU = 4
    wgu = [None] * NWGU
    Rwgu = [None] * NWGU

    def ring_next():
        i = ring_ctr["wgu"] % NWGU
        ring_ctr["wgu"] += 1
        return i

    def alloc_ring(es):
        for i in range(NWGU):
            wgu[i] = sb("wgu%d" % i, [128, DC, 256], BF16, es)
            Rwgu[i] = kb.res("wgu%d" % i)
    NWD = 2
    ring_ctr = {"wgu": 0, "wd": 0, "ps": 0}

    PS = [nc.alloc_psum_tensor("ps%d" % i, [128, 512], F32).ap() for i in range(8)]
    RPS = [Res("ps%d" % i, excl=True) for i in range(8)]

    def next_ps(lo=0, hi=8):
        key = "ps%d_%d" % (lo, hi)
        i = ring_ctr.get(key, 0)
        ring_ctr[key] = i + 1
        b = lo + i % (hi - lo)
        return PS[b], RPS[b]

    def g_col(l, i, c):
        return C1[:, (l * 6 + i) * DC + c:(l * 6 + i) * DC + c + 1]

    def setup_consts():
        kb.op("pool", lambda e: e.memset(identf[:], 0.0), writes=[RC])
        kb.op("pool", lambda e: e.memset(ones1[:], 1.0), writes=[RC])
        kb.op("pool", lambda e: e.memset(onesb[:], 1.0 / 1024.0), writes=[RC])
        kb.op("pool", lambda e: e.memset(epsr[:], RMS_EPS), writes=[RC])
        kb.op("pool", lambda e: e.memset(epsl[:], LN_EPS), writes=[RC])
        kb.op("pool", lambda e: e.memset(onesf[:], 1.0), writes=[RC])
        kb.op("pool", lambda e: e.affine_select(
            out=identf[:], in_=onesf[:], pattern=[[-1, 128]], compare_op=ALU.is_equal,
            fill=0.0, base=0, channel_multiplier=1), reads=[RC], writes=[RC])
        kb.op("dve", lambda e: e.tensor_copy(out=identb[:], in_=identf[:]), reads=[RC], writes=[RC])
        S1 = sb("S1", [116, 128])
        S2 = sb("S2", [124, 128])
        RS = kb.res("cstage")
        kb.dma("sp", S1[0:96, :], norm_g[:, :], writes=[RS])
        RS2 = kb.res("cstage2")
        kb.dma("sp", S1[96:104, :], pool_scale[:, :], writes=[RS2])
        RS3 = kb.res("cstage3")
        kb.dma("sp", S1[104:108, :], conv_dw_b[:, :], writes=[RS3])
        RS4 = kb.res("cstage4")
        kb.dma("sp", S1[108:112, :], conv_ln_g[:, :], writes=[RS4])
        RS5 = kb.res("cstage5")
        kb.dma("sp", S1[112:116, :], conv_ln_b[:, :], writes=[RS5])
        RS6 = kb.res("cstage6")
        kb.dma("sp", S2[:, :], conv_dw_w[:, :], writes=[RS6])
        p, rp = next_ps()
        kb.op("pe", lambda e: e.transpose(out=p[:, 0:116], in_=S1[:, :], identity=identf[0:116, 0:116]),
              reads=[RS, RS2, RS3, RS4, RS5, RC], writes=[rp])
        kb.op("pe", lambda e: e.transpose(out=p[:, 128:252], in_=S2[:, :], identity=identf[0:124, 0:124]),
              reads=[RS6, RC], writes=[rp])
        kb.op("dve", lambda e: e.tensor_copy(out=C1[:], in_=p[:, 0:116]), reads=[rp], writes=[RC])
        kb.op("dve", lambda e: e.tensor_copy(out=C2[:], in_=p[:, 128:252]), reads=[rp], writes=[RC])
        for l in range(2):
            for j, i in enumerate((1, 5)):
                src = C1[:, (l * 6 + i) * DC:(l * 6 + i + 1) * DC]
                dst = ghalf[:, (l * 2 + j) * DC:(l * 2 + j + 1) * DC]
                kb.op("dve", lambda e, s=src, d=dst: e.tensor_scalar_mul(out=d, in0=s, scalar1=0.5),
                      reads=[RC], writes=[RC])

    def load_x(es):
        stg = [sb("xst%d" % i, [128, D], es=es) for i in range(4)]
        Rst = [kb.res("xst%d" % i) for i in range(4)]
        for blk in range(17):
            s, rs = stg[blk % 4], Rst[blk % 4]
            if blk < 16:
                n = 128
                kb.dma("sp", s[:, :], x_prompt[blk * 128:(blk + 1) * 128, :], writes=[rs])
                t = blk // 4
                col = blk * 128
            else:
                n = NSAMP
                kb.dma("sp", s[0:n, :], x_sample[:, :], writes=[rs])
                t = 4
                col = SEQ
            for half in range(2):
                p, rp = next_ps()
                for j in range(4):
                    c = half * 4 + j
                    kb.op("pe", lambda e, p=p, s=s, c=c, j=j, n=n: e.transpose(
                        out=p[:, j * 128:j * 128 + n], in_=s[0:n, c * 128:(c + 1) * 128],
                        identity=identf[0:n, 0:n]), reads=[rs, RC], writes=[rp])
                src = p[:, :].rearrange("p (j n) -> p j n", j=4)[:, :, 0:n]
                dst = X[:, half * 4:half * 4 + 4, col:col + n]
                eng = "act" if half == 0 else "dve"
                if eng == "act":
                    kb.op("act", lambda e, d=dst, s_=src: e.copy(out=d, in_=s_), reads=[rp], writes=[RX[t]])
                else:
                    kb.op("dve", lambda e, d=dst, s_=src: e.tensor_copy(out=d, in_=s_), reads=[rp], writes=[RX[t]])

    def store_x(es):
        stg = [sb("yst%d" % i, [128, D], es=es) for i in range(4)]
        Rst = [kb.res("yst%d" % i) for i in range(4)]
        for blk in range(17):
            s, rs = stg[blk % 4], Rst[blk % 4]
            if blk < 16:
                n, t, col = 128, blk // 4, blk * 128
            else:
                n, t, col = NSAMP, 4, SEQ
            for half in range(2):
                p, rp = next_ps()
                for j in range(4):
                    c = half * 4 + j
                    kb.op("pe", lambda e, p=p, c=c, j=j, n=n, col=col: e.transpose(
                        out=p[0:n, j * 128:(j + 1) * 128], in_=X[:, c, col:col + n],
                        identity=identf[:, :]), reads=[RX[t], RC], writes=[rp])
                dst = s[0:n, half * 512:(half + 1) * 512]
                if half == 0:
                    kb.op("act", lambda e, d=dst, p=p, n=n: e.copy(out=d, in_=p[0:n, :]), reads=[rp], writes=[rs])
                else:
                    kb.op("dve", lambda e, d=dst, p=p, n=n: e.tensor_copy(out=d, in_=p[0:n, :]), reads=[rp], writes=[rs])
            if blk < 16:
                kb.dma("sp", y_prompt[blk * 128:(blk + 1) * 128, :], s[:, :], reads=[rs])
            else:
                kb.dma("sp", y_sample[:, :], s[0:n, :], reads=[rs])

    def norm_stats(src_fn, rsrc, n, rstd, rrstd, sqring):
        p, rp = next_ps(0, 5)
        for c in range(DC):
            sq, rsq = sqring[c % len(sqring)]
            kb.op("act", lambda e, sq=sq, c=c: e.activation(out=sq[:, 0:n], in_=src_fn(c), func=AF.Square),
                  reads=[rsrc], writes=[rsq])
            kb.op("pe", lambda e, sq=sq, c=c: e.matmul(p[:, 0:n], lhsT=onesb[:, :], rhs=sq[:, 0:n],
                                                       start=(c == 0), stop=(c == DC - 1)),
                  reads=[rsq, RC], writes=[rp])
        kb.op("act", lambda e: e.activation(out=rstd[:, 0:n], in_=p[:, 0:n], func=AF.Sqrt, bias=epsr[:, 0:1]),
              reads=[rp, RC], writes=[rrstd])
        kb.op("dve", lambda e: e.reciprocal(out=rstd[:, 0:n], in_=rstd[:, 0:n]), reads=[rrstd], writes=[rrstd])

    def ffn_seq(lfs, es):
        alloc_ring(es)
        wdr = [sb("wd%d" % i, [128, FC, 128], BF16, es=es) for i in range(NWD)]
        Rwd = [kb.res("wd%d" % i) for i in range(NWD)]
        H = sb("H", [128, DC, 1040], BF16, es=es)
        AT = sb("AT", [128, FC, 1040], BF16, es=es)
        Y = sb("Y", [128, DC, 1040], es=es)
        RH = [kb.res("Hs%d" % i) for i in range(3)]
        RAT = [kb.res("ATs%d" % i) for i in range(3)]
        RY = [kb.res("Ys%d" % i) for i in range(3)]
        sqs = [(sb("sq%d" % i, [128, 512], BF16, es=es), kb.res("sq%d" % i)) for i in range(3)]
        sgs = [(sb("sg%d" % i, [128, 512], BF16, es=es), kb.res("sg%d" % i)) for i in range(2)]
        rpre = [(sb("rpre%d" % i, [128, 512], es=es), kb.res("rpre%d" % i)) for i in range(3)]
        rpost = [(sb("rpost%d" % i, [128, 512], es=es), kb.res("rpost%d" % i)) for i in range(3)]
        cnt = {"sg": 0, "sq": 0}
        pending = []

        def drain(k):
            for _ in range(k):
                if pending:
                    pending.pop(0)()

        def slots(M):
            loc = {}
            o = 0
            for si, t in enumerate(M):
                loc[t] = (si, slice(o, o + TILES[t][1]))
                o += TILES[t][1]
            return loc

        def prenorm(l, f, M):
            gi_pre = 0 if f == 0 else 4
            loc = slots(M)
            for t in M:
                si, cs = loc[t]
                n = TILES[t][1]
                rstd, rr = rpre[si]
                norm_stats(lambda c, t=t: X[:, c, ts(t)], RX[t], n, rstd, rr, sqs)
                for c in range(DC):
                    kb.op("dve", lambda e, c=c, t=t, rstd=rstd, n=n, cs=cs: e.scalar_tensor_tensor(
                        out=H[:, c, cs], in0=X[:, c, ts(t)], scalar=g_col(l, gi_pre, c),
                        in1=rstd[:, 0:n], op0=ALU.mult, op1=ALU.mult),
                        reads=[RX[t], rr, RC], writes=[RH[si]])

        def phase1(widx, M):
            wgv = ffn_wg[widx].rearrange("(k p) n -> p k n", p=128)
            wuv = ffn_wu[widx].rearrange("(k p) n -> p k n", p=128)
            loc = slots(M)
            for g in range(FC // 2):
                i = ring_ctr["wgu"]
                ring_ctr["wgu"] += 2
                wg, rwg = wgu[i % NWGU], Rwgu[i % NWGU]
                wu, rwu = wgu[(i + 1) % NWGU], Rwgu[(i + 1) % NWGU]
                kb.dma("pool", wg[:, :, :], wgv[:, :, g * 256:(g + 1) * 256], writes=[rwg])
                kb.dma("pool", wu[:, :, :], wuv[:, :, g * 256:(g + 1) * 256], writes=[rwu])
                for t in M:
                    si, cs = loc[t]
                    n = TILES[t][1]
                    for j in range(2):
                        ch = g * 2 + j
                        pg, rpg = next_ps(0, 5)
                        for k in range(DC):
                            kb.op("pe", lambda e, pg=pg, k=k, j=j, n=n, wg=wg, cs=cs: e.matmul(
                                pg[:, 0:n], lhsT=wg[:, k, j * 128:(j + 1) * 128], rhs=H[:, k, cs],
                                start=(k == 0), stop=(k == DC - 1)), reads=[rwg, RH[si]], writes=[rpg])
                        pu, rpu = next_ps(0, 5)
                        for k in range(DC):
                            kb.op("pe", lambda e, pu=pu, k=k, j=j, n=n, wu=wu, cs=cs: e.matmul(
                                pu[:, 0:n], lhsT=wu[:, k, j * 128:(j + 1) * 128], rhs=H[:, k, cs],
                                start=(k == 0), stop=(k == DC - 1)), reads=[rwu, RH[si]], writes=[rpu])
                        sg, rsg = sgs[cnt["sg"] % 2]
                        cnt["sg"] += 1
                        kb.op("act", lambda e, sg=sg, pg=pg, n=n: e.activation(
                            out=sg[:, 0:n], in_=pg[:, 0:n], func=AF.Silu), reads=[rpg], writes=[rsg])
                        kb.op("dve", lambda e, sg=sg, pu=pu, n=n, ch=ch, cs=cs: e.tensor_tensor(
                            out=AT[:, ch, cs], in0=sg[:, 0:n], in1=pu[:, 0:n], op=ALU.mult),
                            reads=[rsg, rpu], writes=[RAT[si]])
                        drain(1)

        def phase2(widx, M):
            wdv = ffn_wd[widx].rearrange("(k p) n -> p k n", p=128)
            loc = slots(M)
            ssb = {t: (PS[7 - loc[t][0]], RPS[7 - loc[t][0]]) for t in M}
            deferred = []
            for c in range(DC):
                i = ring_ctr["wd"]
                ring_ctr["wd"] += 1
                wd, rwd = wdr[i % NWD], Rwd[i % NWD]
                kb.dma("pool", wd[:, :, :], wdv[:, :, c * 128:(c + 1) * 128], writes=[rwd])
                for t in M:
                    si, cs = loc[t]
                    n = TILES[t][1]
                    py, rpy = next_ps(0, 5)
                    for k in range(FC):
                        kb.op("pe", lambda e, py=py, k=k, n=n, wd=wd, cs=cs: e.matmul(
                            py[:, 0:n], lhsT=wd[:, k, :], rhs=AT[:, k, cs],
                            start=(k == 0), stop=(k == FC - 1)), reads=[rwd, RAT[si]], writes=[rpy])
                    while deferred:
                        deferred.pop(0)()
                    kb.op("dve", lambda e, py=py, c=c, n=n, cs=cs: e.tensor_copy(out=Y[:, c, cs], in_=py[:, 0:n]),
                          reads=[rpy], writes=[RY[si]])
                    sq, rsq = sqs[cnt["sq"] % 3]
                    cnt["sq"] += 1
                    kb.op("act", lambda e, sq=sq, c=c, n=n, cs=cs: e.activation(out=sq[:, 0:n], in_=Y[:, c, cs], func=AF.Square),
                          reads=[RY[si]], writes=[rsq])
                    pss, rpss = ssb[t]
                    deferred.append(lambda pss=pss, rpss=rpss, sq=sq, rsq=rsq, n=n, c=c: kb.op(
                        "pe", lambda e: e.matmul(pss[:, 0:n], lhsT=onesb[:, :], rhs=sq[:, 0:n],
                                                 start=(c == 0), stop=(c == DC - 1)), reads=[rsq, RC], writes=[rpss]))
            while deferred:
                deferred.pop(0)()
            return ssb

        def post_pieces(l, f, M, ssb):
            gj = l * 2 + f
            loc = slots(M)
            for t in M:
                si, cs = loc[t]
                n = TILES[t][1]
                rstd, rr = rpost[si]
                pss, rpss = ssb[t]

                def p0(rstd=rstd, rr=rr, pss=pss, rpss=rpss, n=n):
                    kb.op("act", lambda e: e.activation(out=rstd[:, 0:n], in_=pss[:, 0:n], func=AF.Sqrt, bias=epsr[:, 0:1]),
                          reads=[rpss, RC], writes=[rr])
                    kb.op("dve", lambda e: e.reciprocal(out=rstd[:, 0:n], in_=rstd[:, 0:n]), reads=[rr], writes=[rr])
                pending.append(p0)
                for c in range(DC):
                    def pc(c=c, t=t, si=si, cs=cs, rstd=rstd, rr=rr, n=n):
                        kb.op("dve", lambda e: e.scalar_tensor_tensor(
                            out=Y[:, c, cs], in0=Y[:, c, cs], scalar=ghalf[:, gj * DC + c:gj * DC + c + 1],
                            in1=rstd[:, 0:n], op0=ALU.mult, op1=ALU.mult), reads=[rr, RC, RY[si]], writes=[RY[si]])
                        kb.op("dve", lambda e: e.tensor_tensor(
                            out=X[:, c, ts(t)], in0=X[:, c, ts(t)], in1=Y[:, c, cs], op=ALU.add),
                            reads=[RY[si], RX[t]], writes=[RX[t]])
                    pending.append(pc)

        steps = [(l, f, M) for (l, f) in lfs for M in MACROS]
        prenorm(*steps[0])
        for i, (l, f, M) in enumerate(steps):
            widx = l * 2 + f
            phase1(widx, M)
            drain(len(pending))
            if i + 1 < len(steps):
                prenorm(*steps[i + 1])
            ssb = phase2(widx, M)
            post_pieces(l, f, M, ssb)
        drain(len(pending))

    def post_tile(Yt, rY, t, n, gi, l, pss, rpss, rstd, rr):
        kb.op("act", lambda e: e.activation(out=rstd[:, 0:n], in_=pss[:, 0:n], func=AF.Sqrt, bias=epsr[:, 0:1]),
              reads=[rpss, RC], writes=[rr])
        kb.op("dve", lambda e: e.reciprocal(out=rstd[:, 0:n], in_=rstd[:, 0:n]), reads=[rr], writes=[rr])
        for c in range(DC):
            kb.op("dve", lambda e, c=c: e.scalar_tensor_tensor(
                out=Yt[:, c, 0:n], in0=Yt[:, c, 0:n], scalar=g_col(l, gi, c), in1=rstd[:, 0:n],
                op0=ALU.mult, op1=ALU.mult), reads=[rr, RC], writes=[rY])
            kb.op("pool", lambda e, c=c: e.tensor_tensor(
                out=X[:, c, ts(t)], in0=X[:, c, ts(t)], in1=Yt[:, c, 0:n], op=ALU.add),
                reads=[rY, RX[t]], writes=[RX[t]])

    def load_w(dst, rdst, src):
        kb.dma("pool", dst, src, writes=[rdst])


    def gen_sample(es, QT, RQT, KT, RKT, Vs, RVs, LFs, RLFs, attT, Ratt, Rm, triu, tris, onesf2):
        pti = sb("pti", [128, 4 * NPAGES], I32, es)
        idx = sb("idx", [128, 4 * NPAGES], I32, es)
        iop = sb("iop", [128, 1], F32, es)
        Rix = kb.res("idx")
        kb.dma("sp", pti[:, :], page_table.rearrange("b j -> (b j)").partition_broadcast(128), writes=[Rix])
        kb.op("pool", lambda e: e.iota(out=iop[:, :], pattern=[[0, 1]], base=0, channel_multiplier=1,
                                       allow_small_or_imprecise_dtypes=True), writes=[Rix])
        kb.op("dve", lambda e: e.tensor_scalar(out=idx[:, :], in0=pti[:, :], scalar1=128.0, scalar2=iop[:, 0:1],
                                               op0=ALU.mult, op1=ALU.add), reads=[Rix], writes=[Rix])
        Qblk = sb("Qblk", [128, 4, 4, 8], BF16, es)
        RQb = kb.res("Qblk")
        kb.op("dve", lambda e: e.memset(Qblk[:], 0.0), writes=[RQb])
        for pr in range(4):
            kb.op("dve", lambda e, pr=pr: e.tensor_copy(
                out=Qblk[0:64, pr, :, 0:4], in_=QT[0:64, pr, SEQ:SEQ + 16].rearrange("p (b q) -> p b q", b=4)),
                reads=[RQT[4]], writes=[RQb])
            kb.op("dve", lambda e, pr=pr: e.tensor_copy(
                out=Qblk[64:128, pr, :, 4:8], in_=QT[64:128, pr, SEQ:SEQ + 16].rearrange("p (b q) -> p b q", b=4)),
                reads=[RQT[4]], writes=[RQb])
        mask4 = sb("mask4", [4, 4], F32, es)
        negt4 = sb("negt4", [4, 4], F32, es)
        kb.op("pool", lambda e: e.memset(negt4[:], NEG * 8.0), writes=[RQb])
        kb.op("pool", lambda e: e.affine_select(out=mask4[:], in_=negt4[:], pattern=[[-1, 4]], compare_op=ALU.is_gt,
                                                fill=0.0, base=0, channel_multiplier=1), reads=[RQb], writes=[RQb])
        Lb1 = sb("Lb", [128, NPAGES, NH], F32, es)
        RLb = kb.res("Lb")
        Lpg = sb("Lpg", [NPAGES, PAGE * NH], F32, es)
        RLpg = kb.res("Lpg")
        idxp = sb("idxp", [NPAGES, 4], I32, es)
        Ridxp = kb.res("idxp")
        for b_ in range(4):
            kb.dma("sp", idxp[:, b_:b_ + 1], page_table[b_:b_ + 1, :].rearrange("o j -> j o"), writes=[Ridxp])
        clp = cache_logf.rearrange("(n t) h -> n (t h)", t=PAGE)
        Bb = sb("Bb", [128, NPAGES, NH, 1], F32, es)
        RB = kb.res("Bb")
        Ra = sb("Ra", [128, NPAGES + 32, NH], F32, es)
        Rb2 = sb("Rb2", [128, NPAGES + 32, NH], F32, es)
        RRa = kb.res("Ra")
        RRb = kb.res("Rb2")
        kb.op("dve", lambda e: e.memset(Ra[:], 0.0), writes=[RRa])
        kb.op("dve", lambda e: e.memset(Rb2[:], 0.0), writes=[RRb])
        Kp = [(sb("Kp%d" % i, [128, DATT], BF16, es), kb.res("Kp%d" % i)) for i in range(2)]
        Vp = [(sb("Vp%d" % i, [128, DATT], BF16, es), kb.res("Vp%d" % i)) for i in range(4)]
        KTp = [(sb("KTp%d" % i, [128, DATT], BF16, es), kb.res("KTp%d" % i)) for i in range(2)]
        Sb = [(sb("Sb%d" % i, [128, 32], F32, es), kb.res("Sb%d" % i)) for i in range(2)]
        Pb = [(sb("Pb%d" % i, [128, 128], BF16, es), kb.res("Pb%d" % i)) for i in range(2)]
        for i in range(2):
            kb.op("dve", lambda e, i=i: e.memset(Pb[i][0][:], 0.0), writes=[Pb[i][1]])
        biasn = sb("biasn", [4, NH, 4], F32, es)
        negcn = sb("negcn", [4, NH, 1], F32, es)
        Rbn = kb.res("biasn")
        rd = sb("rd", [32, 2], F32, es)
        Pacc = sb("Pacc", [128, 32], F32, es)
        RPacc = kb.res("Pacc")
        On = sb("On", [32, DATT], F32, es)
        ROn = kb.res("On")
        cl = cache_logf
        def logf_gather(bb):
            kb.dma("pool", None, None, fn=lambda e: e.indirect_dma_start(
                out=Lpg[:, :], out_offset=None, in_=clp[:, :],
                in_offset=bass.IndirectOffsetOnAxis(ap=idxp[:, bb:bb + 1], axis=0)),
                reads=[Ridxp], writes=[RLpg], owner=RLpg)

        logf_gather(0)
        for b in range(4):
            plt, rplt = next_ps(5, 7)
            for h in range(NH):
                kb.op("pe", lambda e, h=h: e.transpose(out=plt[:, h * NPAGES:(h + 1) * NPAGES], in_=Lpg[:, h:PAGE * NH:NH],
                                                       identity=identf[0:NPAGES, 0:NPAGES]), reads=[RLpg, RC], writes=[rplt])
            kb.op("dve", lambda e: e.tensor_copy(out=Lb1[:, :, :], in_=plt[:, :].rearrange("p (h j) -> p j h", h=NH)),
                  reads=[rplt], writes=[RLb])
            if b + 1 < 4:
                logf_gather(b + 1)
            RLq = [RLb]
            pw, rpw = next_ps(5, 7)
            ptot, rptot = next_ps(5, 7)
            Lf = Lb1[:, :, :].rearrange("p j h -> p (j h)")
            kb.op("pe", lambda e: e.matmul(pw[:, :], lhsT=tris[:, :], rhs=Lf, start=True, stop=True), reads=RLq + [Rm], writes=[rpw])
            kb.op("pe", lambda e: e.matmul(ptot[:, :], lhsT=onesf2[:, :], rhs=Lf, start=True, stop=True), reads=RLq + [Rm], writes=[rptot])
            kb.op("dve", lambda e: e.tensor_copy(out=Ra[:, 0:NPAGES - 1, :], in_=ptot[:, NH:NPAGES * NH].rearrange("p (j h) -> p j h", h=NH)),
                  reads=[rptot], writes=[RRa])
            cur, rcur, nxt, rnxt = Ra, RRa, Rb2, RRb
            for sh in (1, 2, 4, 8, 16, 32):
                kb.op("dve", lambda e, cur=cur, nxt=nxt, sh=sh: e.tensor_tensor(
                    out=nxt[:, 0:NPAGES, :], in0=cur[:, 0:NPAGES, :], in1=cur[:, sh:NPAGES + sh, :], op=ALU.add),
                    reads=[rcur], writes=[rnxt])
                cur, rcur, nxt, rnxt = nxt, rnxt, cur, rcur
            kb.op("dve", lambda e, cur=cur: e.tensor_tensor(
                out=Bb[:, :, :, 0], in0=pw[:, :].rearrange("p (j h) -> p j h", h=NH), in1=cur[:, 0:NPAGES, :], op=ALU.add),
                reads=[rpw, rcur], writes=[RB])
            kb.op("dve", lambda e: e.memset(Ra[:, NPAGES - 1:, :], 0.0), reads=[RRb], writes=[RRa])
            pcn, rpcn = next_ps(5, 7)
            kb.op("pe", lambda e, b=b: e.matmul(pcn[0:4, 0:NH], lhsT=triu[0:4, 0:4], rhs=LFs[0:4, b, :], start=True, stop=True),
                  reads=[RLFs, Rm], writes=[rpcn])
            kb.op("dve", lambda e: e.tensor_scalar_mul(out=negcn[:, :, 0], in0=pcn[0:4, 0:NH], scalar1=-1.0), reads=[rpcn], writes=[Rbn])
            kb.op("dve", lambda e: e.tensor_tensor(out=biasn[:, :, :], in0=negcn[:, :, :].to_broadcast([4, NH, 4]),
                                                   in1=mask4[:, :].rearrange("s (o t) -> s o t", o=1).to_broadcast([4, NH, 4]),
                                                   op=ALU.add), reads=[Rbn, RQb], writes=[Rbn])
            po, rpo = PS[7], RPS[7]
            kb.op("dve", lambda e: e.memset(Pacc[:], 0.0), writes=[RPacc])
            NKP, NVP = len(Kp), len(Vp)

            def st_gather(j):
                kp, rkp = Kp[j % NKP]
                vp, rvp = Vp[j % NVP]
                off = bass.IndirectOffsetOnAxis(ap=idx[:, b * NPAGES + j:b * NPAGES + j + 1], axis=0)
                kb.dma("pool", None, None, fn=lambda e: e.indirect_dma_start(
                    out=kp[:, :], out_offset=None, in_=cache_k[:, :], in_offset=off), reads=[Rix], writes=[rkp], owner=rkp)
                kb.dma("pool", None, None, fn=lambda e: e.indirect_dma_start(
                    out=vp[:, :], out_offset=None, in_=cache_v[:, :], in_offset=off), reads=[Rix], writes=[rvp], owner=rvp)

            def st_transpose(j):
                kp, rkp = Kp[j % NKP]
                pk, rpk = next_ps(5, 7)
                pkb = pk[:, :].bitcast(BF16)
                for pr in range(4):
                    kb.op("pe", lambda e, pr=pr: e.transpose(
                        out=pkb[:, pr * 128:(pr + 1) * 128], in_=kp[:, pr * 128:(pr + 1) * 128], identity=identb[:, :]),
                        reads=[rkp, RC], writes=[rpk])
                ktp, rktp = KTp[j % 2]
                kb.op("act", lambda e: e.copy(out=ktp[:, :], in_=pkb[:, 0:DATT]), reads=[rpk], writes=[rktp])

            def st_scores(j):
                last = (j == NPAGES)
                ps_, rps_ = next_ps(5, 7)
                sbt, rsb = Sb[j % 2]
                pbt, rpb = Pb[j % 2]
                if not last:
                    ktp, rktp = KTp[j % 2]
                    for pr in range(4):
                        kb.op("pe", lambda e, pr=pr: e.matmul(
                            ps_[:, pr * 8:(pr + 1) * 8], lhsT=ktp[:, pr * 128:(pr + 1) * 128], rhs=Qblk[:, pr, b, :],
                            start=True, stop=True), reads=[rktp, RQb], writes=[rps_])
                    nk = 128
                    bias_ap = Bb[:, j, :, :].to_broadcast([128, NH, 4])
                    rbias = RB
                else:
                    for pr in range(4):
                        kb.op("pe", lambda e, pr=pr: e.matmul(
                            ps_[0:4, pr * 8:(pr + 1) * 8], lhsT=KT[:, pr, SEQ + 4 * b:SEQ + 4 * b + 4], rhs=Qblk[:, pr, b, :],
                            start=True, stop=True), reads=[RKT[4], RQb], writes=[rps_])
                    nk = 4
                    bias_ap = biasn[:, :, :]
                    rbias = Rbn
                kb.op("dve", lambda e: e.scalar_tensor_tensor(
                    out=sbt[0:nk, :].rearrange("p (h q) -> p h q", h=NH), in0=ps_[0:nk, 0:32].rearrange("p (h q) -> p h q", h=NH),
                    scalar=0.125, in1=bias_ap, op0=ALU.mult, op1=ALU.add), reads=[rps_, rbias], writes=[rsb])
                kb.op("act", lambda e: e.activation(out=pbt[0:nk, 0:32], in_=sbt[0:nk, :], func=AF.Exp),
                      reads=[rsb], writes=[rpb])

            def st_pv(j):
                last = (j == NPAGES)
                nk = 4 if last else 128
                pbt, rpb = Pb[j % 2]
                if last:
                    rhs_v, rrv = Vs[0:4, b, :], RVs
                else:
                    vp, rvp = Vp[j % NVP]
                    rhs_v, rrv = vp[:, :], rvp
                kb.op("pe", lambda e: e.matmul(
                    po[:, :], lhsT=pbt[0:nk, :], rhs=rhs_v, start=(j == 0), stop=last), reads=[rpb, rrv], writes=[rpo])
                kb.op("dve", lambda e: e.tensor_tensor(
                    out=Pacc[0:nk, :], in0=Pacc[0:nk, :], in1=pbt[0:nk, 0:32], op=ALU.add), reads=[rpb, RPacc], writes=[RPacc])

            for step in range(NPAGES + 4):
                if step < NPAGES:
                    st_gather(step)
                if 0 <= step - 1 < NPAGES:
                    st_transpose(step - 1)
                if 0 <= step - 2 <= NPAGES:
                    st_scores(step - 2)
                if 0 <= step - 3 <= NPAGES:
                    st_pv(step - 3)
                yield
            pd, rpd = next_ps(5, 7)
            kb.op("pe", lambda e, pd=pd: e.matmul(pd[0:32, 0:2], lhsT=Pacc[:, 0:32], rhs=onesf[:, 0:2], start=True, stop=True),
                  reads=[RPacc, RC], writes=[rpd])
            kb.op("dve", lambda e, pd=pd: e.reciprocal(out=rd[:, :], in_=pd[0:32, 0:2]), reads=[rpd], writes=[ROn])
            kb.op("dve", lambda e: e.tensor_scalar_mul(out=On[:, :], in0=po[0:32, :], scalar1=rd[:, 0:1]), reads=[rpo, ROn], writes=[ROn])
            for pr in range(4):
                ptr, rptr = next_ps(5, 7)
                kb.op("pe", lambda e, ptr=ptr, pr=pr: e.transpose(out=ptr[:, 0:32], in_=On[0:32, pr * 128:(pr + 1) * 128],
                                                                  identity=identf[0:32, 0:32]), reads=[ROn, RC], writes=[rptr])
                for hl in range(2):
                    h = 2 * pr + hl
                    kb.op("dve", lambda e, ptr=ptr, pr=pr, hl=hl, h=h, b=b: e.tensor_copy(
                        out=attT[64 * hl:64 * hl + 64, pr, SEQ + 4 * b:SEQ + 4 * b + 4], in_=ptr[64 * hl:64 * hl + 64, h * 4:h * 4 + 4]),
                        reads=[rptr], writes=[Ratt[4]])


    def gen_conv(es, UT, RUT, UTtail, RUTt, UTs, RUTs, cvT, Rcv, ones512, Rm):
        Yc = sb("Yc", [128, 4, 512], F32, es)
        RYc = kb.res("Yc")
        tmp = sb("ctmp", [128, 512], F32, es)
        Rtmp = kb.res("ctmp")
        mean = sb("cmean", [128, 512], F32, es)
        Rmean = kb.res("cmean")
        crs, Rcrs = tmp, Rtmp
        scst = sb("scst", [30, DCONV], F32, es)
        Rscst = kb.res("scst")
        dg = [(sb("dg%d" % i, [128, 128], BF16, es), kb.res("dg%d" % i)) for i in range(6)]
        PB, RPB = PS[4], RPS[4]
        for b in range(4):
            kb.dma("sp", scst[:, :], state_conv[b], writes=[Rscst])
            for j in range(4):
                kb.op("pe", lambda e, j=j: e.transpose(out=PB[:, j * 32:j * 32 + 30], in_=scst[0:30, j * 128:(j + 1) * 128],
                                                       identity=identf[0:30, 0:30]), reads=[Rscst, RC], writes=[RPB])
            kb.op("dve", lambda e, b=b: e.tensor_copy(
                out=UTs[:, :, b, 0:30], in_=PB[:, 0:128].rearrange("p (j n) -> p j n", j=4)[:, :, 0:30]),
                reads=[RPB], writes=[RUTs])
            rcp = kb.res("cpst%d" % b)
            kb.dma("sp", o_cs[b, 0:26, :], state_conv[b, 4:30, :], writes=[rcp])
            yield

        def ln_swish(nt, dst, rdst):
            for j in range(4):
                kb.op("pe", lambda e, j=j: e.matmul(PB[:, 0:nt], lhsT=ones512[:, :], rhs=Yc[:, j, 0:nt],
                                                    start=(j == 0), stop=(j == 3)), reads=[RYc, Rm], writes=[RPB])
            kb.op("dve", lambda e: e.tensor_copy(out=mean[:, 0:nt], in_=PB[:, 0:nt]), reads=[RPB], writes=[Rmean])
            for j in range(4):
                kb.op("act", lambda e, j=j: e.activation(out=tmp[:, 0:nt], in_=Yc[:, j, 0:nt], func=AF.Square),
                      reads=[RYc], writes=[Rtmp])
                kb.op("pe", lambda e, j=j: e.matmul(PB[:, 0:nt], lhsT=ones512[:, :], rhs=tmp[:, 0:nt],
                                                    start=(j == 0), stop=(j == 3)), reads=[Rtmp, Rm], writes=[RPB])
            kb.op("dve", lambda e: e.tensor_tensor(out=crs[:, 0:nt], in0=mean[:, 0:nt], in1=mean[:, 0:nt], op=ALU.mult),
                  reads=[Rmean], writes=[Rcrs])
            kb.op("dve", lambda e: e.tensor_tensor(out=crs[:, 0:nt], in0=PB[:, 0:nt], in1=crs[:, 0:nt], op=ALU.subtract),
                  reads=[RPB, Rcrs], writes=[Rcrs])
            kb.op("act", lambda e: e.activation(out=crs[:, 0:nt], in_=crs[:, 0:nt], func=AF.Sqrt, bias=epsl[:, 0:1]),
                  reads=[Rcrs, RC], writes=[Rcrs])
            kb.op("dve", lambda e: e.reciprocal(out=crs[:, 0:nt], in_=crs[:, 0:nt]), reads=[Rcrs], writes=[Rcrs])
            for j in range(4):
                kb.op("dve", lambda e, j=j: e.tensor_tensor(out=Yc[:, j, 0:nt], in0=Yc[:, j, 0:nt], in1=mean[:, 0:nt], op=ALU.subtract),
                      reads=[Rmean, RYc], writes=[RYc])
                kb.op("dve", lambda e, j=j: e.tensor_tensor(out=Yc[:, j, 0:nt], in0=Yc[:, j, 0:nt], in1=crs[:, 0:nt], op=ALU.mult),
                      reads=[Rcrs, RYc], writes=[RYc])
                kb.op("act", lambda e, j=j: e.activation(out=dst(j), in_=Yc[:, j, 0:nt], func=AF.Silu,
                                                         scale=C1[:, 108 + j:109 + j], bias=C1[:, 112 + j:113 + j]),
                      reads=[RYc, RC], writes=[rdst])

        ndg = 0
        for t in range(4):
            for j in range(4):
                for tap in range(CONVW):
                    d_, rd_ = dg[ndg % 6]
                    ndg += 1
                    kb.op("dve", lambda e, d_=d_, tap=tap, j=j: e.tensor_scalar_mul(
                        out=d_[:, :], in0=identb[:, :], scalar1=C2[:, tap * 4 + j:tap * 4 + j + 1]), reads=[RC], writes=[rd_])
                    kb.op("pe", lambda e, d_=d_, tap=tap, j=j, t=t: e.matmul(
                        PB[:, :], lhsT=d_[:, :], rhs=UT[:, j, t * 512 + tap:t * 512 + tap + 512],
                        start=(tap == 0), stop=(tap == CONVW - 1)),
                        reads=[rd_, RUT[t]] + ([RUT[t - 1]] if t > 0 else []), writes=[RPB])
                    if tap % 8 == 7:
                        yield
                kb.op("act", lambda e, j=j: e.activation(out=Yc[:, j, :], in_=PB[:, :], func=AF.Identity,
                                                         bias=C1[:, 104 + j:105 + j]), reads=[RPB, RC], writes=[RYc])
                yield
            ln_swish(512, lambda j, t=t: cvT[:, j, ts(t)], Rcv[t])
            yield
        for j in range(4):
            kb.op("dve", lambda e, j=j: e.tensor_scalar(
                out=Yc[:, j, 0:16].rearrange("p (b q) -> p b q", b=4), in0=UTs[:, j, :, 0:4], scalar1=C2[:, j:j + 1],
                scalar2=C1[:, 104 + j:105 + j], op0=ALU.mult, op1=ALU.add), reads=[RUTs, RC], writes=[RYc])
            for tap in range(1, CONVW):
                kb.op("dve", lambda e, j=j, tap=tap: e.scalar_tensor_tensor(
                    out=Yc[:, j, 0:16].rearrange("p (b q) -> p b q", b=4), in0=UTs[:, j, :, tap:tap + 4],
                    scalar=C2[:, tap * 4 + j:tap * 4 + j + 1], in1=Yc[:, j, 0:16].rearrange("p (b q) -> p b q", b=4),
                    op0=ALU.mult, op1=ALU.add), reads=[RUTs, RC, RYc], writes=[RYc])
            yield
        ln_swish(16, lambda j: cvT[:, j, SEQ:SEQ + 16], Rcv[4])
        yield
        ost, Rost = scst, Rscst
        for b in range(5):
            for j in range(4):
                src = UTtail[:, j, :] if b == 4 else UTs[:, j, b, 30:34]
                nn = 30 if b == 4 else 4
                kb.op("pe", lambda e, j=j, src=src, nn=nn: e.transpose(
                    out=PB[0:nn, j * 128:(j + 1) * 128], in_=src, identity=identf[:, :]),
                    reads=[RUTt if b == 4 else RUTs, RC], writes=[RPB])
            kb.op("dve", lambda e, nn=nn: e.tensor_copy(out=ost[0:nn, :], in_=PB[0:nn, :]), reads=[RPB], writes=[Rost])
            if b == 4:
                kb.dma("sp", o_cp[:, :], ost[0:30, :], reads=[Rost])
            else:
                kb.dma("sp", o_cs[b, 26:30, :], ost[0:4, :], reads=[Rost])
            yield

    def gen_prompt(esc, QT, RQT, KT, RKT, Vx, RV, negc, Rnegc, maskneg, Rm, attT, Ratt):
        PTs = [(sb("PT%d" % i, [128, 512], BF16, esc), kb.res("PT%d" % i)) for i in range(2)]
        Qz = [(sb("Qz%d" % i, [128, 512], BF16, esc), kb.res("Qz%d" % i)) for i in range(2)]
        att_tok = sb("att_tok", [128, 4, 128], BF16, esc)
        Rat = kb.res("att_tok")
        rdn = sb("rdn", [128, 8, 1], F32, esc)
        Rrdn = kb.res("rdn")
        for hl in range(2):
            kb.op("dve", lambda e, hl=hl: e.memset(Qz[hl][0][:], 0.0), writes=[Qz[hl][1]])
        npt = 0
        for pr in range(4):
            for qt in range(4):
                q0 = qt * 512
                for hl in range(2):
                    rows = slice(64 * hl, 64 * hl + 64)
                    kb.op("dve", lambda e, hl=hl, rows=rows: e.tensor_copy(out=Qz[hl][0][rows, :], in_=QT[rows, pr, q0:q0 + 512]),
                          reads=[RQT[qt]], writes=[Qz[hl][1]])
                pO = [(PS[2], RPS[2]), (PS[3], RPS[3])]
                pOv = [p[0][:, 0:264].rearrange("p (q c) -> p q c", c=66) for p in pO]
                first = [True, True]
                nkt = 4 * qt + 4
                steps = [(kt, hl) for kt in range(nkt) for hl in range(2)]

                def emit_s(kt, hl):
                    nonlocal npt
                    jd = kt - 4 * qt
                    qs = max(jd, 0) * 128
                    n = 512 - qs
                    h = 2 * pr + hl
                    ps_, rps_ = next_ps(0, 2)
                    kb.op("pe", lambda e: e.matmul(
                        ps_[:, 0:n], lhsT=KT[:, pr, kt * 128:(kt + 1) * 128], rhs=Qz[hl][0][:, qs:512],
                        start=True, stop=(jd < 0)), reads=[RKT[kt // 4], Qz[hl][1]], writes=[rps_])
                    if jd >= 0:
                        kb.op("pe", lambda e: e.matmul(ps_[:, 0:128], lhsT=identb[:, :], rhs=maskneg[:, :],
                                                       start=False, stop=True), reads=[Rm, RC], writes=[rps_])
                    pt, rpt = PTs[npt % 2]
                    npt += 1
                    kb.op("act", lambda e: e.activation(
                        out=pt[:, 0:n], in_=ps_[:, 0:n], func=AF.Exp, scale=0.125, bias=negc[:, kt, h:h + 1]),
                        reads=[rps_, Rnegc], writes=[rpt])
                    return (kt, hl, jd, qs, h, pt, rpt)

                def emit_pv(info):
                    kt, hl, jd, qs, h, pt, rpt = info
                    for qb in range(max(jd, 0), 4):
                        lo = qb * 128 - qs
                        st_ = first[hl]
                        first[hl] = False
                        kb.op("pe", lambda e, qb=qb, lo=lo, st_=st_: e.matmul(
                            pOv[hl][:, qb, :], lhsT=pt[:, lo:lo + 128], rhs=Vx[:, kt, h, :],
                            start=st_, stop=(kt == 4 * qt + qb), skip_group_check=True),
                            reads=[rpt, RV[kt]], writes=[pO[hl][1]])

                prev = emit_s(*steps[0])
                for si in range(1, len(steps) + 1):
                    cur = emit_s(*steps[si]) if si < len(steps) else None
                    emit_pv(prev)
                    prev = cur
                    if si % 2 == 0:
                        yield
                for hl in range(2):
                    kb.op("dve", lambda e, hl=hl: e.reciprocal(out=rdn[:, hl * 4:(hl + 1) * 4, :], in_=pOv[hl][:, :, 64:65]),
                          reads=[pO[hl][1]], writes=[Rrdn])
                    kb.op("dve", lambda e, hl=hl: e.tensor_tensor(
                        out=att_tok[:, :, hl * 64:(hl + 1) * 64], in0=pOv[hl][:, :, 0:64],
                        in1=rdn[:, hl * 4:(hl + 1) * 4, :].to_broadcast([128, 4, 64]), op=ALU.mult),
                        reads=[pO[hl][1], Rrdn], writes=[Rat])
                ptb, rptb = next_ps(0, 2)
                ptbb = ptb[:, :].bitcast(BF16)
                for qb in range(4):
                    kb.op("pe", lambda e, qb=qb, ptbb=ptbb: e.transpose(out=ptbb[:, qb * 128:(qb + 1) * 128], in_=att_tok[:, qb, :],
                                                                       identity=identb[:, :]), reads=[Rat, RC], writes=[rptb])
                kb.op("act", lambda e, ptbb=ptbb, pr=pr, q0=q0: e.copy(out=attT[:, pr, q0:q0 + 512], in_=ptbb[:, 0:512]),
                      reads=[rptb], writes=[Ratt[qt]])
                yield

    def mixer0(es):
        l = 0
        QT = sb("QT", [128, 4, NTOK], BF16, es)
        KT = sb("KT", [128, 4, NTOK], BF16, es)
        RQT = [kb.res("QT%d" % t) for t in range(5)]
        RKT = [kb.res("KT%d" % t) for t in range(5)]
        Vx = sb("Vx", [128, 16, NH, 66], BF16, es)
        RV = [kb.res("V%d" % b) for b in range(16)]
        kb.op("pool", lambda e: e.memset(Vx[:, :, :, 64:65], 1.0), writes=RV)
        kb.op("pool", lambda e: e.memset(Vx[:, :, :, 65:66], 0.0), writes=RV)
        Vs = sb("Vs", [4, 4, DATT], BF16, es)
        RVs = kb.res("Vs")
        cvT = sb("cvT", [128, 4, NTOK], BF16, es)
        Rcv = [kb.res("cv%d" % t) for t in range(5)]
        attT = QT
        Ratt = RQT
        LF = sb("LF", [128, 16, NH], F32, es)
        RLF = kb.res("LF")
        LFs = sb("LFs", [4, 4, NH], F32, es)
        RLFs = kb.res("LFs")
        negc = sb("negc", [128, 16, NH], F32, es)
        Rnegc = kb.res("negc")
        fgb = sb("fgb", [128, NH], F32, es)
        Rm = kb.res("mixc")
        kb.dma("sp", fgb[:, :], fgate_b.partition_broadcast(128), writes=[Rm])
        triu = sb("triu", [128, 128], F32, es)
        tris = sb("tris", [128, 128], F32, es)
        onesf2 = onesf
        ones512 = sb("ones512", [128, 128], F32, es)
        maskneg = sb("maskneg", [128, 128], BF16, es)
        negt = ones512
        kb.op("pool", lambda e: e.memset(negt[:], NEG), reads=[RC], writes=[Rm])
        kb.op("pool", lambda e: e.affine_select(out=triu[:], in_=onesf2[:], pattern=[[1, 128]], compare_op=ALU.is_ge,
                                                fill=0.0, base=0, channel_multiplier=-1), reads=[Rm], writes=[Rm])
        kb.op("pool", lambda e: e.affine_select(out=tris[:], in_=onesf2[:], pattern=[[-1, 128]], compare_op=ALU.is_gt,
                                                fill=0.0, base=0, channel_multiplier=1), reads=[Rm], writes=[Rm])
        kb.op("pool", lambda e: e.affine_select(out=maskneg[:], in_=negt[:], pattern=[[-1, 128]], compare_op=ALU.is_gt,
                                                fill=0.0, base=0, channel_multiplier=1), reads=[Rm], writes=[Rm])
        kb.op("pool", lambda e: e.memset(ones512[:], 1.0 / 512.0), reads=[Rm], writes=[Rm])
        win = mix_w_in.rearrange("(k p) n -> p k n", p=128)

        with ExitStack() as esu:
            UT = sb("UT", [128, 4, 30 + SEQ], BF16, esu)
            UTtail = sb("UTtail", [128, 4, 30], F32, esu)
            RUTt = kb.res("UTtail")
            RUT = [kb.res("UT%d" % t) for t in range(4)]
            UTs = sb("UTs", [128, 4, 4, 34], F32, esu)
            RUTs = kb.res("UTs")
            kb.op("pool", lambda e: e.memset(UT[:, :, 0:30], 0.0), writes=[RUT[0]])
            with ExitStack() as esa:
                alloc_ring(esa)
                Hb = [sb("Hm%d" % i, [128, DC, 512], BF16, esa) for i in range(2)]
                RHb = [kb.res("Hmb%d" % i) for i in range(2)]
                stg = [(sb("kvst%d" % i, [128, DATT], F32, esa), kb.res("kvst%d" % i)) for i in range(1)]
                lst = [(sb("lfst%d" % i, [128, 2 * NH], F32, esa), kb.res("lfst%d" % i)) for i in range(2)]
                sqs = [(sb("msq%d" % i, [128, 512], BF16, esa), kb.res("msq%d" % i)) for i in range(2)]
                sig = [(sb("sig%d" % i, [128, 512], F32, esa), kb.res("sig%d" % i)) for i in range(1)]
                rstd, rr = sb("mrstd", [128, 512], F32, esa), kb.res("mrstd")
                cnt = {"s": 0, "l": 0, "g": 0}
                def emit_norm(t):
                    n = TILES[t][1]
                    Hn, rHn = Hb[t % 2], RHb[t % 2]
                    norm_stats(lambda c, t=t: X[:, c, ts(t)], RX[t], n, rstd, rr, sqs)
                    for c in range(DC):
                        kb.op("dve", lambda e, c=c, t=t, n=n: e.scalar_tensor_tensor(
                            out=Hn[:, c, 0:n], in0=X[:, c, ts(t)], scalar=g_col(l, 2, c),
                            in1=rstd[:, 0:n], op0=ALU.mult, op1=ALU.mult),
                            reads=[RX[t], rr, RC], writes=[rHn])

                emit_norm(0)
                for mi, M in enumerate([[0], [1], [2], [3], [4]]):
                    H = Hb[mi % 2]
                    RH = {t: RHb[mi % 2] for t in M}
                    loc = {}
                    o = 0
                    for t in M:
                        loc[t] = slice(o, o + TILES[t][1])
                        o += TILES[t][1]
                    blocks = []
                    for t in M:
                        if t < 4:
                            for bb in range(4):
                                blocks.append((t, loc[t].start + bb * 128, 128, "p", t * 4 + bb))
                        else:
                            for b in range(4):
                                blocks.append((t, loc[t].start + b * 4, 4, "s", b))

                    def feat_proj(col0, dst, rdst):
                        s0, s1 = ring_next(), ring_next()
                        load_w(wgu[s0][:, :, :], Rwgu[s0], win[:, :, col0:col0 + 256])
                        load_w(wgu[s1][:, :, :], Rwgu[s1], win[:, :, col0 + 256:col0 + 512])
                        sls = (s0, s1)
                        for t in M:
                            n = TILES[t][1]
                            for j in range(4):
                                w, rw = wgu[sls[j // 2]], Rwgu[sls[j // 2]]
                                p, rp = next_ps(0, 6)
                                for k in range(DC):
                                    kb.op("pe", lambda e, p=p, k=k, j=j, w=w, t=t, n=n: e.matmul(
                                        p[:, 0:n], lhsT=w[:, k, (j % 2) * 128:(j % 2) * 128 + 128], rhs=H[:, k, loc[t]],
                                        start=(k == 0), stop=(k == DC - 1)), reads=[rw, RH[t]], writes=[rp])
                                kb.op("act", lambda e, p=p, j=j, t=t, n=n: e.copy(out=dst[:, j, ts(t)], in_=p[:, 0:n]),
                                      reads=[rp], writes=[rdst[t]])
                        return sls

                    def tok_proj(col0, which, sl=None):
                        if sl is None:
                            sl = (ring_next(), ring_next())
                            load_w(wgu[sl[0]][:, :, :], Rwgu[sl[0]], win[:, :, col0:col0 + 256])
                            load_w(wgu[sl[1]][:, :, :], Rwgu[sl[1]], win[:, :, col0 + 256:col0 + 512])
                        for (t, c0, nb, kind, idx) in blocks:
                            p, rp = next_ps(0, 6)
                            for hf in range(2):
                                w, rw = wgu[sl[hf]], Rwgu[sl[hf]]
                                for k in range(DC):
                                    kb.op("pe", lambda e, p=p, k=k, w=w, c0=c0, nb=nb, hf=hf: e.matmul(
                                        p[0:nb, hf * 256:(hf + 1) * 256], lhsT=H[:, k, c0:c0 + nb], rhs=w[:, k, :],
                                        start=(k == 0), stop=(k == DC - 1)), reads=[rw, RH[t]], writes=[rp])
                            st, rst = stg[0]
                            kb.op("dve", lambda e, st=st, p=p, nb=nb: e.tensor_copy(out=st[0:nb, :], in_=p[0:nb, :]),
                                  reads=[rp], writes=[rst])
                            if which == "v":
                                if kind == "p":
                                    kb.op("act", lambda e, st=st, idx=idx: e.copy(
                                        out=Vx[:, idx, :, 0:64], in_=st[:, :].rearrange("p (h d) -> p h d", h=NH)),
                                        reads=[rst], writes=[RV[idx]])
                                else:
                                    kb.op("act", lambda e, st=st, idx=idx: e.copy(out=Vs[0:4, idx, :], in_=st[0:4, :]),
                                          reads=[rst], writes=[RVs])
                            if kind == "p":
                                dst = (o_kp if which == "k" else o_vp)[idx * 128:(idx + 1) * 128, :]
                            else:
                                dst = (o_ks if which == "k" else o_vs)[idx * 4:(idx + 1) * 4, :]
                            kb.dma("sp", dst, st[0:nb, :], reads=[rst])

                    feat_proj(0, QT, RQT)
                    if mi + 1 < 5:
                        emit_norm(mi + 1)
                    ksl = feat_proj(512, KT, RKT)
                    tok_proj(512, "k", ksl)
                    tok_proj(1024, "v")
                    fs = ring_next()
                    load_w(wgu[fs][:, :, 0:NH], Rwgu[fs], win[:, :, 1536:1536 + NH])
                    for (t, c0, nb, kind, idx) in blocks:
                        p, rp = next_ps(0, 6)
                        for k in range(DC):
                            kb.op("pe", lambda e, p=p, k=k, c0=c0, nb=nb: e.matmul(
                                p[0:nb, 0:NH], lhsT=H[:, k, c0:c0 + nb], rhs=wgu[fs][:, k, 0:NH],
                                start=(k == 0), stop=(k == DC - 1)), reads=[Rwgu[fs], RH[t]], writes=[rp])
                        st, rst = lst[cnt["l"] % 2]
                        cnt["l"] += 1
                        kb.op("dve", lambda e, st=st, p=p, nb=nb: e.tensor_tensor(
                            out=st[0:nb, 0:NH], in0=p[0:nb, 0:NH], in1=fgb[0:nb, :], op=ALU.add),
                            reads=[rp, Rm], writes=[rst])
                        kb.op("act", lambda e, st=st, nb=nb: e.activation(out=st[0:nb, 0:NH], in_=st[0:nb, 0:NH],
                                                                        func=AF.Exp, scale=-1.0), reads=[rst], writes=[rst])
                        kb.op("act", lambda e, st=st, nb=nb: e.activation(out=st[0:nb, 0:NH], in_=st[0:nb, 0:NH],
                                                                        func=AF.Ln, bias=1.0), reads=[rst], writes=[rst])
                        if kind == "p":
                            kb.op("dve", lambda e, st=st, idx=idx: e.tensor_scalar_mul(out=LF[:, idx, :], in0=st[:, 0:NH], scalar1=-1.0),
                                  reads=[rst], writes=[RLF])
                            kb.op("dve", lambda e, st=st, idx=idx: e.tensor_copy(out=st[:, NH:2 * NH], in_=LF[:, idx, :]),
                                  reads=[RLF], writes=[rst])
                            kb.dma("sp", o_lp[idx * 128:(idx + 1) * 128, :], st[:, NH:2 * NH], reads=[rst])
                        else:
                            kb.op("dve", lambda e, st=st, idx=idx: e.tensor_scalar_mul(out=LFs[0:4, idx, :], in0=st[0:4, 0:NH], scalar1=-1.0),
                                  reads=[rst], writes=[RLFs])
                            kb.op("dve", lambda e, st=st, idx=idx: e.tensor_copy(out=st[0:4, NH:2 * NH], in_=LFs[0:4, idx, :]),
                                  reads=[RLFs], writes=[rst])
                            kb.dma("sp", o_ls[idx * 4:(idx + 1) * 4, :], st[0:4, NH:2 * NH], reads=[rst])
                    gsl = [ring_next() for _ in range(4)]
                    for i in range(2):
                        load_w(wgu[gsl[i]][:, :, :], Rwgu[gsl[i]], win[:, :, 1544 + i * 256:1544 + (i + 1) * 256])
                        load_w(wgu[gsl[2 + i]][:, :, :], Rwgu[gsl[2 + i]], win[:, :, 2056 + i * 256:2056 + (i + 1) * 256])
                    for t in M:
                        n = TILES[t][1]
                        for j in range(4):
                            pa, rpa = next_ps(0, 6)
                            pg, rpg = next_ps(0, 6)
                            for (p, rp, base) in ((pa, rpa, 0), (pg, rpg, 2)):
                                w, rw = wgu[gsl[base + j // 2]], Rwgu[gsl[base + j // 2]]
                                for k in range(DC):
                                    kb.op("pe", lambda e, p=p, k=k, j=j, w=w, t=t, n=n: e.matmul(
                                        p[:, 0:n], lhsT=w[:, k, (j % 2) * 128:(j % 2) * 128 + 128], rhs=H[:, k, loc[t]],
                                        start=(k == 0), stop=(k == DC - 1)), reads=[rw, RH[t]], writes=[rp])
                            sg, rsg = sig[0]
                            cnt["g"] += 1
                            kb.op("act", lambda e, sg=sg, pg=pg, n=n: e.activation(out=sg[:, 0:n], in_=pg[:, 0:n], func=AF.Sigmoid),
                                  reads=[rpg], writes=[rsg])
                            if t < 4:
                                kb.op("dve", lambda e, sg=sg, pa=pa, j=j, t=t: e.tensor_tensor(
                                    out=UT[:, j, 30 + t * 512:30 + (t + 1) * 512], in0=pa[:, 0:512], in1=sg[:, 0:512], op=ALU.mult),
                                    reads=[rpa, rsg], writes=[RUT[t]])
                                if t == 3:
                                    kb.op("dve", lambda e, sg=sg, pa=pa, j=j: e.tensor_tensor(
                                        out=UTtail[:, j, :], in0=pa[:, 482:512], in1=sg[:, 482:512], op=ALU.mult),
                                        reads=[rpa, rsg], writes=[RUTt])
                            else:
                                kb.op("dve", lambda e, sg=sg, pa=pa, j=j: e.tensor_tensor(
                                    out=UTs[:, j, :, 30:34], in0=pa[:, 0:16].rearrange("p (b q) -> p b q", b=4),
                                    in1=sg[:, 0:16].rearrange("p (b q) -> p b q", b=4), op=ALU.mult),
                                    reads=[rpa, rsg], writes=[RUTs])
                lacc = sb("lacc", [128, NH], F32, esa)
                Rla = kb.res("lacc")
                kb.op("dve", lambda e: e.memset(lacc[:], 0.0), writes=[Rla])
                for blk in range(16):
                    p, rp = next_ps(0, 6)
                    kb.op("pe", lambda e, p=p, blk=blk: e.matmul(p[:, 0:NH], lhsT=triu[:, :], rhs=LF[:, blk, :], start=True, stop=False),
                          reads=[RLF, Rm], writes=[rp])
                    kb.op("pe", lambda e, p=p: e.matmul(p[:, 0:NH], lhsT=onesf2[:, :], rhs=lacc[:, :], start=False, stop=True),
                          reads=[Rla, Rm], writes=[rp])
                    kb.op("dve", lambda e, p=p, blk=blk: e.tensor_scalar_mul(out=negc[:, blk, :], in0=p[:, 0:NH], scalar1=-1.0),
                          reads=[rp], writes=[Rnegc])
                    kb.op("dve", lambda e, blk=blk: e.tensor_tensor(out=lacc[:], in0=lacc[:], in1=LF[:, blk, :], op=ALU.add),
                          reads=[RLF, Rla], writes=[Rla])
                kb.barrier()
            with ExitStack() as esg:
                gens = [gen_prompt(esg, QT, RQT, KT, RKT, Vx, RV, negc, Rnegc, maskneg, Rm, attT, Ratt),
                        gen_sample(esg, QT, RQT, KT, RKT, Vs, RVs, LFs, RLFs, attT, Ratt, Rm, triu, tris, onesf2),
                        gen_conv(esg, UT, RUT, UTtail, RUTt, UTs, RUTs, cvT, Rcv, ones512, Rm)]
                weights = [1, 1, 1]
                alive = [True, True, True]
                rnd = 0
                while any(alive):
                    for gi_, g in enumerate(gens):
                        if not alive[gi_]:
                            continue
                        if gi_ == 2 and rnd % 3 != 0:
                            continue
                        for _ in range(weights[gi_]):
                            try:
                                next(g)
                            except StopIteration:
                                alive[gi_] = False
                                break
                    rnd += 1
                kb.barrier()
        with ExitStack() as esd:
            alloc_ring(esd)
            Yts = [sb("Ymix%d" % i, [128, DC, 512], F32, esd) for i in range(2)]
            RYts = [kb.res("Ymix%d" % i) for i in range(2)]
            sqs = [(sb("dsq%d" % i, [128, 512], BF16, esd), kb.res("dsq%d" % i)) for i in range(3)]
            rstds = [(sb("drstd%d" % i, [128, 512], F32, esd), kb.res("drstd%d" % i)) for i in range(2)]
            wo = mix_w_out.rearrange("(k p) n -> p k n", p=128)
            for i in range(4):
                load_w(wgu[i][:, :, :], Rwgu[i], wo[:, :, i * 256:(i + 1) * 256])
            nsq = [0]

            def d_mm(t):
                n = TILES[t][1]
                Yt, RYt = Yts[t % 2], RYts[t % 2]
                pss, rpss = PS[7 - t % 2], RPS[7 - t % 2]
                deferred = []
                for c in range(DC):
                    w, rw = wgu[c // 2], Rwgu[c // 2]
                    p, rp = next_ps(0, 6)
                    for k in range(DC):
                        src, rsrc = (attT, Ratt) if k < 4 else (cvT, Rcv)
                        kb.op("pe", lambda e, p=p, k=k, c=c, w=w, src=src: e.matmul(
                            p[:, 0:n], lhsT=w[:, k, (c % 2) * 128:(c % 2) * 128 + 128], rhs=src[:, k % 4, ts(t)],
                            start=(k == 0), stop=(k == DC - 1)), reads=[rw, rsrc[t]], writes=[rp])
                    while deferred:
                        deferred.pop(0)()
                    kb.op("dve", lambda e, p=p, c=c: e.tensor_copy(out=Yt[:, c, 0:n], in_=p[:, 0:n]), reads=[rp], writes=[RYt])
                    sq, rsq = sqs[nsq[0] % 3]
                    nsq[0] += 1
                    kb.op("act", lambda e, sq=sq, c=c: e.activation(out=sq[:, 0:n], in_=Yt[:, c, 0:n], func=AF.Square),
                          reads=[RYt], writes=[rsq])
                    deferred.append(lambda sq=sq, rsq=rsq, c=c: kb.op(
                        "pe", lambda e: e.matmul(pss[:, 0:n], lhsT=onesb[:, :], rhs=sq[:, 0:n],
                                                 start=(c == 0), stop=(c == DC - 1)), reads=[rsq, RC], writes=[rpss]))
                while deferred:
                    deferred.pop(0)()

            def d_post(t):
                post_tile(Yts[t % 2], RYts[t % 2], t, TILES[t][1], 3, l, PS[7 - t % 2], RPS[7 - t % 2],
                          rstds[t % 2][0], rstds[t % 2][1])

            d_mm(0)
            for t in range(1, 5):
                d_mm(t)
                d_post(t - 1)
            d_post(4)

    def mixer1(es):
        l = 1
        Rm = kb.res("m1c")
        invc = sb("invc", [128, 15], F32, es)
        for pos in range(15):
            kb.op("pool", lambda e, pos=pos: e.memset(invc[:, pos:pos + 1], 1.0 / (pos + 1)), writes=[Rm])
        alloc_ring(es)
        for gi in range(4):
            load_w(wgu[0][:, 2 * gi:2 * gi + 2, :], Rwgu[0], pool_w[gi].rearrange("(ci p) n -> p ci n", p=128))
        PW, RPW = wgu[0], Rwgu[0]
        L = 15 + 512
        HFs = [sb("HF%d" % i, [128, DC, L], F32, es) for i in range(2)]
        RHF = [kb.res("HF%d" % i) for i in range(2)]
        A = sb("WA", [128, DC, L], F32, es)
        B = sb("WB", [128, 6, L], F32, es)
        RA, RB_ = kb.res("WA"), kb.res("WB")
        MXs = [sb("MX%d" % i, [128, DC, 512], BF16, es) for i in range(2)]
        RMXs = [kb.res("MX%d" % i) for i in range(2)]
        Yts = [sb("Ypool%d" % i, [128, DC, 512], F32, es) for i in range(2)]
        RYts = [kb.res("Ypool%d" % i) for i in range(2)]
        rposts = [(sb("prpost%d" % i, [128, 512], F32, es), kb.res("prpost%d" % i)) for i in range(2)]
        sqs = [(sb("psq%d" % i, [128, 512], BF16, es), kb.res("psq%d" % i)) for i in range(2)]
        rstd, rr = sb("prstd", [128, 512], F32, es), kb.res("prstd")
        stg = sb("pstg", [15, D], F32, es)
        Rstg = kb.res("pstg")
        nsq = [0]

        def window_mix(HF, rHF, G, Lg, first, par):
            MX, RMX = MXs[par], RMXs[par]
            n = Lg - 15
            A4 = A[:, :, 0:G * Lg].rearrange("p c (g l) -> p c g l", g=G)
            B4 = B[:, :, 0:G * Lg].rearrange("p c (g l) -> p c g l", g=G)
            kb.op("dve", lambda e: e.tensor_tensor(out=A4[:, :, :, 1:Lg], in0=HF[:, :, :, 1:Lg], in1=HF[:, :, :, 0:Lg - 1], op=ALU.add),
                  reads=[rHF], writes=[RA])
            kb.op("dve", lambda e: e.tensor_tensor(out=B4[:, 0:6, :, 3:Lg], in0=A4[:, 2:8, :, 3:Lg], in1=A4[:, 2:8, :, 1:Lg - 2], op=ALU.add),
                  reads=[RA], writes=[RB_])
            kb.op("dve", lambda e: e.tensor_tensor(out=A4[:, 4:8, :, 7:Lg], in0=B4[:, 2:6, :, 7:Lg], in1=B4[:, 2:6, :, 3:Lg - 4], op=ALU.add),
                  reads=[RB_, RA], writes=[RA])
            kb.op("dve", lambda e: e.tensor_tensor(out=B4[:, 4:6, :, 15:Lg], in0=A4[:, 6:8, :, 15:Lg], in1=A4[:, 6:8, :, 7:Lg - 8], op=ALU.add),
                  reads=[RA, RB_], writes=[RB_])
            for c in range(DC):
                gi = c // 2
                w = 2 << gi
                src = A4[:, c] if gi in (0, 2) else B4[:, c - 2]
                mx = MX[:, c, 0:G * n].rearrange("p (g n) -> p g n", g=G)
                kb.op("dve", lambda e, src=src, mx=mx, c=c, w=w: e.scalar_tensor_tensor(
                    out=mx, in0=src[:, :, 15:Lg], scalar=1.0 / w, in1=HF[:, c, :, 15:Lg], op0=ALU.mult, op1=ALU.subtract),
                    reads=[RA, RB_, rHF], writes=[RMX])
                if first and w > 1:
                    kb.op("dve", lambda e, src=src, c=c, w=w: e.tensor_tensor(
                        out=A4[:, c, :, 0:w - 1] if gi in (1, 3) else B4[:, 0, :, 0:w - 1], in0=src[:, :, 15:15 + w - 1],
                        in1=invc[:, 0:w - 1].rearrange("p (g n) -> p g n", g=1), op=ALU.mult), reads=[RA, RB_, Rm], writes=[RA, RB_])
                    tmpv = A4[:, c, :, 0:w - 1] if gi in (1, 3) else B4[:, 0, :, 0:w - 1]
                    kb.op("dve", lambda e, tmpv=tmpv, mx=mx, c=c, w=w: e.tensor_tensor(
                        out=mx[:, :, 0:w - 1], in0=tmpv, in1=HF[:, c, :, 15:15 + w - 1], op=ALU.subtract),
                        reads=[RA, RB_, rHF], writes=[RMX])

        def proj_mm(t, n, par):
            MX, RMX = MXs[par], RMXs[par]
            Yt, RYt = Yts[par], RYts[par]
            pss, rpss = PS[7 - par], RPS[7 - par]
            for c in range(DC):
                gi, no = c // 2, c % 2
                p, rp = next_ps(0, 6)
                for ci in range(2):
                    kb.op("pe", lambda e, p=p, gi=gi, no=no, ci=ci: e.matmul(
                        p[:, 0:n], lhsT=PW[:, 2 * gi + ci, no * 128:(no + 1) * 128], rhs=MX[:, 2 * gi + ci, 0:n],
                        start=(ci == 0), stop=(ci == 1)), reads=[RPW, RMX], writes=[rp])
                kb.op("act", lambda e, p=p, c=c: e.activation(out=Yt[:, c, 0:n], in_=p[:, 0:n], func=AF.Copy,
                                                              scale=C1[:, 96 + c:97 + c]), reads=[rp, RC], writes=[RYt])
                sq, rsq = sqs[nsq[0] % 2]
                nsq[0] += 1
                kb.op("act", lambda e, sq=sq, c=c: e.activation(out=sq[:, 0:n], in_=Yt[:, c, 0:n], func=AF.Square),
                      reads=[RYt], writes=[rsq])
                kb.op("pe", lambda e, sq=sq, c=c: e.matmul(pss[:, 0:n], lhsT=onesb[:, :], rhs=sq[:, 0:n],
                                                           start=(c == 0), stop=(c == DC - 1)), reads=[rsq, RC], writes=[rpss])

        def proj_post(t, n, par):
            post_tile(Yts[par], RYts[par], t, n, 3, l, PS[7 - par], RPS[7 - par], rposts[par][0], rposts[par][1])

        kb.op("dve", lambda e: e.memset(HFs[0][:, :, 0:15], 0.0), writes=[RHF[0]])
        def stage_a(t):
            HF, rHF = HFs[t % 2], RHF[t % 2]
            norm_stats(lambda c, t=t: X[:, c, ts(t)], RX[t], 512, rstd, rr, sqs)
            for c in range(DC):
                kb.op("dve", lambda e, c=c, t=t, HF=HF: e.scalar_tensor_tensor(
                    out=HF[:, c, 15:L], in0=X[:, c, ts(t)], scalar=g_col(l, 2, c), in1=rstd[:, 0:512],
                    op0=ALU.mult, op1=ALU.mult), reads=[RX[t], rr, RC], writes=[rHF])
            if t < 3:
                kb.op("act", lambda e, HF=HF, t=t: e.copy(out=HFs[(t + 1) % 2][:, :, 0:15], in_=HF[:, :, 512:L]),
                      reads=[rHF], writes=[RHF[(t + 1) % 2]])
            else:
                for half in range(2):
                    p, rp = next_ps(0, 6)
                    for j in range(4):
                        c = half * 4 + j
                        kb.op("pe", lambda e, p=p, c=c, j=j, HF=HF: e.transpose(out=p[0:15, j * 128:(j + 1) * 128], in_=HF[:, c, 512:L],
                                                                               identity=identf[:, :]), reads=[rHF, RC], writes=[rp])
                    kb.op("dve", lambda e, p=p, half=half: e.tensor_copy(out=stg[0:15, half * 512:(half + 1) * 512], in_=p[0:15, :]),
                          reads=[rp], writes=[Rstg])
                kb.dma("sp", o_pp[:, :], stg[0:15, :], reads=[Rstg])
            window_mix(HF[:, :, :].rearrange("p c (g l) -> p c g l", g=1), rHF, 1, L, t == 0, t % 2)

        stage_a(0)
        for t in range(4):
            proj_mm(t, 512, t % 2)
            if t < 3:
                stage_a(t + 1)
            proj_post(t, 512, t % 2)
        Ls = 19
        HS = HFs[0][:, :, 0:4 * Ls].rearrange("p c (g l) -> p c g l", g=4)
        rHS = RHF[0]
        sst, Rsst = stg, Rstg
        for b in range(4):
            kb.dma("sp", sst[:, :], state_pool[b], writes=[Rsst])
            for half in range(2):
                p, rp = next_ps(0, 6)
                for j in range(4):
                    c = half * 4 + j
                    kb.op("pe", lambda e, p=p, c=c, j=j: e.transpose(out=p[:, j * 32:j * 32 + 15], in_=sst[0:15, c * 128:(c + 1) * 128],
                                                                   identity=identf[0:15, 0:15]), reads=[Rsst, RC], writes=[rp])
                kb.op("dve", lambda e, p=p, half=half, b=b: e.tensor_copy(
                    out=HS[:, half * 4:half * 4 + 4, b, 0:15], in_=p[:, 0:128].rearrange("p (j n) -> p j n", j=4)[:, :, 0:15]),
                    reads=[rp], writes=[rHS])
            rcp = kb.res("cpsp%d" % b)
            kb.dma("sp", o_ps[b, 0:11, :], state_pool[b, 4:15, :], writes=[rcp])
        norm_stats(lambda c: X[:, c, SEQ:SEQ + 16], RX[4], 16, rstd, rr, sqs)
        for c in range(DC):
            kb.op("dve", lambda e, c=c: e.scalar_tensor_tensor(
                out=HS[:, c, :, 15:19], in0=X[:, c, SEQ:SEQ + 16].rearrange("p (b q) -> p b q", b=4), scalar=g_col(l, 2, c),
                in1=rstd[:, 0:16].rearrange("p (b q) -> p b q", b=4), op0=ALU.mult, op1=ALU.mult),
                reads=[RX[4], rr, RC], writes=[rHS])
        for b in range(4):
            for half in range(2):
                p, rp = next_ps(0, 6)
                for j in range(4):
                    c = half * 4 + j
                    kb.op("pe", lambda e, p=p, c=c, j=j, b=b: e.transpose(out=p[0:4, j * 128:(j + 1) * 128], in_=HS[:, c, b, 15:19],
                                                                         identity=identf[:, :]), reads=[rHS, RC], writes=[rp])
                kb.op("dve", lambda e, p=p, half=half: e.tensor_copy(out=stg[0:4, half * 512:(half + 1) * 512], in_=p[0:4, :]),
                      reads=[rp], writes=[Rstg])
            kb.dma("sp", o_ps[b, 11:15, :], stg[0:4, :], reads=[Rstg])
        window_mix(HS, rHS, 4, Ls, False, 0)
        proj_mm(4, 16, 0)
        proj_post(4, 16, 0)

    def phase(fn, *a):
        with ExitStack() as es:
            fn(*a, es)
            kb.barrier()

    setup_consts()
    phase(load_x)
    if stage >= 1:
        phase(ffn_seq, [(0, 0)])
    if stage >= 2:
        phase(mixer0)
    if stage >= 4:
        phase(ffn_seq, [(0, 1), (1, 0)])
    if stage >= 5:
        phase(mixer1)
    if stage >= 6:
        phase(ffn_seq, [(1, 1)])
    phase(store_x)
    kb.finish()
    return nc


_CACHE = {}


def _get_nc(stage, npool=NPOOLPG):
    if (stage, npool) not in _CACHE:
        _CACHE[(stage, npool)] = build(stage, npool)
    return _CACHE[(stage, npool)]


def kernel(x_prompt, x_sample, cache_k, cache_v, cache_logf, state_conv, state_pool, page_table,
           norm_g, ffn_w_gate, ffn_w_up, ffn_w_down, mix_w_in, fgate_b, conv_dw_w, conv_dw_b,
           conv_ln_g, conv_ln_b, mix_w_out, pool_w, pool_scale, _stage=None):
    stage = int(os.environ.get("KSTAGE", "99")) if _stage is None else _stage
    npool = int(np.asarray(cache_k).shape[1])
    ncores = int(os.environ.get("KCORES", str(NCORES)))
    nc = _get_nc(stage, npool)
    f = lambda a: np.ascontiguousarray(np.asarray(a))
    shared = {
        "cache_k": f(cache_k).reshape(npool * PAGE, DATT),
        "cache_v": f(cache_v).reshape(npool * PAGE, DATT),
        "cache_logf": f(cache_logf).reshape(npool * PAGE, NH),
        "norm_g": f(norm_g).reshape(12 * DC, 128),
        "ffn_w_gate": f(ffn_w_gate).reshape(4, D, DFF),
        "ffn_w_up": f(ffn_w_up).reshape(4, D, DFF),
        "ffn_w_down": f(ffn_w_down).reshape(4, DFF, D),
        "mix_w_in": f(mix_w_in).reshape(D, DIN),
        "fgate_b": f(fgate_b).reshape(NH),
        "conv_dw_w": f(conv_dw_w).reshape(CONVW * 4, 128),
        "conv_dw_b": f(conv_dw_b).reshape(4, 128),
        "conv_ln_g": f(conv_ln_g).reshape(4, 128),
        "conv_ln_b": f(conv_ln_b).reshape(4, 128),
        "mix_w_out": f(mix_w_out).reshape(D, D),
        "pool_w": f(pool_w).reshape(4, 256, 256),
        "pool_scale": f(pool_scale).reshape(DC, 128),
    }
    xp = f(x_prompt)
    xs = f(x_sample)
    sc = f(state_conv)
    spl = f(state_pool)
    pt = f(page_table)
    in_maps = []
    for c in range(ncores):
        m = dict(shared)
        m["x_prompt"] = xp[c]
        m["x_sample"] = xs[4 * c:4 * c + 4].reshape(NSAMP, D)
        m["state_conv"] = sc[0, 4 * c:4 * c + 4]
        m["state_pool"] = spl[0, 4 * c:4 * c + 4]
        m["page_table"] = pt[4 * c:4 * c + 4]
        in_maps.append(m)
    if os.environ.get("KTRACE"):
        res = run_bass_kernel_spmd(nc, in_maps, core_ids=list(range(ncores)), trace=True)
        print("KTRACE exec_time_ns", res.exec_time_ns)
    else:
        res = run_bass_kernel_spmd(nc, in_maps, core_ids=list(range(ncores)))
    R = list(res.results) + [res.results[0]] * (NCORES - ncores)
    cat = lambda k: np.stack([np.asarray(r[k]) for r in R])
    y_p = cat("y_prompt")
    y_s = cat("y_sample").reshape(32, 4, D)
    kp = cat("new_k_prompt").reshape(1, 8, SEQ, NH, HD)
    vp = cat("new_v_prompt").reshape(1, 8, SEQ, NH, HD)
    lp = cat("new_logf_prompt").reshape(1, 8, SEQ, NH)
    cp = cat("new_conv_prompt").reshape(1, 8, 30, DCONV)
    pp = cat("new_pool_prompt").reshape(1, 8, 15, D)
    ks = cat("new_k_sample").reshape(1, 32, 4, NH, HD)
    vs = cat("new_v_sample").reshape(1, 32, 4, NH, HD)
    ls = cat("new_logf_sample").reshape(1, 32, 4, NH)
    cs = cat("new_conv_sample").reshape(1, 32, 30, DCONV)
    ps = cat("new_pool_sample").reshape(1, 32, 15, D)
    return (y_p, y_s, kp, vp, lp, cp, pp, ks, vs, ls, cs, ps)
```

```python
import os
import bisect
from contextlib import ExitStack
import numpy as np
import concourse.bass as bass
import concourse.mybir as mybir
from concourse.bass_utils import run_bass_kernel_spmd

F32 = mybir.dt.float32
BF16 = mybir.dt.bfloat16
I32 = mybir.dt.int32
U32 = mybir.dt.uint32
AF = mybir.ActivationFunctionType
ALU = mybir.AluOpType

NCORES = 8
D = 1024
DC = 8
DFF = 2816
FC = 22
SEQ = 2048
NSAMP = 16
NTOK = SEQ + NSAMP
TILES = [(0, 512), (512, 512), (1024, 512), (1536, 512), (2048, 16)]
MACROS = [[0, 1], [2, 3, 4]]
NH = 8
HD = 64
DATT = 512
DCONV = 512
CONVW = 31
DIN = 2568
PAST = 8192
PAGE = 128
NPAGES = 64
NPOOLPG = 2560
RMS_EPS = 1e-6
LN_EPS = 1e-5
NEG = -30000.0


class Tk:
    __slots__ = ("eng", "seq")

    def __init__(self, eng, seq):
        self.eng = eng
        self.seq = seq


class Res:
    __slots__ = ("name", "w", "r", "dsem", "dval", "excl")

    def __init__(self, name, excl=False):
        self.name = name
        self.excl = excl
        self.w = None
        self.r = {}
        self.dsem = None
        self.dval = 0


class Eng:
    def __init__(self, name, handle, sem):
        self.name = name
        self.h = handle
        self.sem = sem
        self.count = 0
        self.seq = 0
        self.last = None
        self.last_inced = True
        self.inc_seqs = []
        self.inc_tickets = []
        self.known = {}


class KB:
    def __init__(self):
        self.nc = bass.Bass("TRN2", target_bir_lowering=False)
        nc = self.nc
        self.E = {}
        for nm, h in (("pe", nc.tensor), ("act", nc.scalar), ("dve", nc.vector),
                      ("pool", nc.gpsimd), ("sp", nc.sync)):
            self.E[nm] = Eng(nm, h, nc.alloc_semaphore("sem_" + nm))
        self.pending_dma = {}
        self.nres = 0

    def res(self, name):
        return Res(name)

    def _resolve(self, tk):
        e = self.E[tk.eng]
        i = bisect.bisect_left(e.inc_seqs, tk.seq)
        if i < len(e.inc_seqs):
            return e.sem, e.inc_tickets[i]
        assert e.last is not None and not e.last_inced
        e.last.then_inc(e.sem, 1)
        e.count += 1
        e.last_inced = True
        e.inc_seqs.append(e.seq)
        e.inc_tickets.append(e.count)
        return e.sem, e.count

    def _wait(self, E, sem, val):
        e = self.E[E]
        k = id(sem)
        if e.known.get(k, 0) >= val:
            return
        e.h.wait_ge(sem, val)
        e.known[k] = val

    def _need(self, E, dep):
        if dep is None:
            return
        if isinstance(dep, Tk):
            if dep.eng == E and E in ("pe", "sp"):
                return
            sem, val = self._resolve(dep)
            self._wait(E, sem, val)
        else:
            _, sem, val = dep
            self._wait(E, sem, val)

    def _deps(self, E, reads, writes):
        for r in reads:
            self._need(E, r.w)
        for w in writes:
            self._need(E, w.w)
            for v in w.r.values():
                self._need(E, v)

    def op(self, E, fn, reads=(), writes=()):
        e = self.E[E]
        if any(r.excl for r in reads):
            writes = list(writes) + [r for r in reads if r.excl]
            reads = [r for r in reads if not r.excl]
        self._deps(E, reads, writes)
        ins = fn(e.h)
        e.seq += 1
        e.last = ins
        e.last_inced = False
        tk = Tk(E, e.seq)
        for r in reads:
            r.r[E] = tk
        for w in writes:
            w.w = tk
            w.r = {}
        return ins

    def dma(self, Q, out, in_, reads=(), writes=(), owner=None, fn=None):
        e = self.E[Q]
        if e.seq and not e.last_inced:
            self._resolve(Tk(Q, e.seq))
        self._deps(Q, reads, writes)
        own = owner or (writes[0] if writes else reads[0])
        if own.dsem is None:
            self.nres += 1
            own.dsem = self.nc.alloc_semaphore("d%d_%s" % (self.nres, own.name))
        if own.dval:
            self._wait(Q, own.dsem, own.dval)
        ins = fn(e.h) if fn is not None else e.h.dma_start(out=out, in_=in_)
        own.dval += 16
        ins.then_inc(own.dsem, 16)
        e.seq += 1
        e.last = ins
        e.last_inced = True
        dep = ("dma", own.dsem, own.dval)
        for r in reads:
            r.r[("dma", id(own.dsem))] = dep
        for w in writes:
            w.w = dep
            w.r = {}
        self.pending_dma[id(own.dsem)] = (own.dsem, own.dval)
        return ins

    def barrier(self, engines=("pe", "act", "dve", "pool", "sp")):
        tks = {}
        for nm in ("pe", "act", "dve", "pool"):
            e = self.E[nm]
            if e.seq:
                if not e.last_inced:
                    tks[nm] = self._resolve(Tk(nm, e.seq))
                elif e.inc_seqs:
                    tks[nm] = (e.sem, e.inc_tickets[-1])
        for F in engines:
            for nm, (sem, val) in tks.items():
                if nm != F:
                    self._wait(F, sem, val)
            for sem, val in self.pending_dma.values():
                self._wait(F, sem, val)
        self.pending_dma = {}

    def finish(self):
        self.barrier(engines=("sp",))


def ts(t):
    o, n = TILES[t]
    return slice(o, o + n)


def build(stage, NPOOLPG=NPOOLPG):
    kb = KB()
    nc = kb.nc

    def din(name, shape, dt=F32):
        return nc.dram_tensor(name, list(shape), dt, kind="ExternalInput").ap()

    def dout(name, shape, dt=F32):
        return nc.dram_tensor(name, list(shape), dt, kind="ExternalOutput").ap()

    x_prompt = din("x_prompt", [SEQ, D])
    x_sample = din("x_sample", [NSAMP, D])
    cache_k = din("cache_k", [NPOOLPG * PAGE, DATT])
    cache_v = din("cache_v", [NPOOLPG * PAGE, DATT])
    cache_logf = din("cache_logf", [NPOOLPG * PAGE, NH])
    state_conv = din("state_conv", [4, 30, DCONV])
    state_pool = din("state_pool", [4, 15, D])
    page_table = din("page_table", [4, NPAGES], I32)
    norm_g = din("norm_g", [12 * DC, 128])
    ffn_wg = din("ffn_w_gate", [4, D, DFF])
    ffn_wu = din("ffn_w_up", [4, D, DFF])
    ffn_wd = din("ffn_w_down", [4, DFF, D])
    mix_w_in = din("mix_w_in", [D, DIN])
    fgate_b = din("fgate_b", [NH])
    conv_dw_w = din("conv_dw_w", [CONVW * 4, 128])
    conv_dw_b = din("conv_dw_b", [4, 128])
    conv_ln_g = din("conv_ln_g", [4, 128])
    conv_ln_b = din("conv_ln_b", [4, 128])
    mix_w_out = din("mix_w_out", [D, D])
    pool_w = din("pool_w", [4, 256, 256])
    pool_scale = din("pool_scale", [DC, 128])

    y_prompt = dout("y_prompt", [SEQ, D])
    y_sample = dout("y_sample", [NSAMP, D])
    o_kp = dout("new_k_prompt", [SEQ, DATT])
    o_vp = dout("new_v_prompt", [SEQ, DATT])
    o_lp = dout("new_logf_prompt", [SEQ, NH])
    o_cp = dout("new_conv_prompt", [30, DCONV])
    o_pp = dout("new_pool_prompt", [15, D])
    o_ks = dout("new_k_sample", [NSAMP, DATT])
    o_vs = dout("new_v_sample", [NSAMP, DATT])
    o_ls = dout("new_logf_sample", [NSAMP, NH])
    o_cs = dout("new_conv_sample", [4, 30, DCONV])
    o_ps = dout("new_pool_sample", [4, 15, D])

    uniq = [0]

    def sb(name, shape, dt=F32, es=None):
        if es is None:
            return nc.alloc_sbuf_tensor(name, list(shape), dt).ap()
        uniq[0] += 1
        return es.enter_context(nc.sbuf_tensor("%s_u%d" % (name, uniq[0]), list(shape), dt)).ap()

    X = sb("X", [128, DC, NTOK])
    RX = [kb.res("X%d" % t) for t in range(5)]
    identf = sb("identf", [128, 128])
    identb = sb("identb", [128, 128], BF16)
    onesb = sb("onesb", [128, 128], BF16)
    ones1 = sb("ones1", [128, 128], BF16)
    C1 = sb("C1", [128, 116])
    C2 = sb("C2", [128, 124])
    ghalf = sb("ghalf", [128, 4 * DC])
    onesf = sb("onesf", [128, 128])
    epsr = sb("epsr", [128, 1])
    epsl = sb("epsl", [128, 1])
    RC = kb.res("consts")
    NWGU = 4
    wgu = [None] * NWGU
    Rwgu = [None] * NWGU

    def ring_next():
        i = ring_ctr["wgu"] % NWGU
        ring_ctr["wgu"] += 1
        return i

    def alloc_ring(es):
        for i in range(NWGU):
            wgu[i] = sb("wgu%d" % i, [128, DC, 256], BF16, es)
            Rwgu[i] = kb.res("wgu%d" % i)
    NWD = 2
    ring_ctr = {"wgu": 0, "wd": 0, "ps": 0}

    PS = [nc.alloc_psum_tensor("ps%d" % i, [128, 512], F32).ap() for i in range(8)]
    RPS = [Res("ps%d" % i, excl=True) for i in range(8)]

    def next_ps(lo=0, hi=8):
        key = "ps%d_%d" % (lo, hi)
        i = ring_ctr.get(key, 0)
        ring_ctr[key] = i + 1
        b = lo + i % (hi - lo)
        return PS[b], RPS[b]

    def g_col(l, i, c):
        return C1[:, (l * 6 + i) * DC + c:(l * 6 + i) * DC + c + 1]

    def setup_consts():
        kb.op("pool", lambda e: e.memset(identf[:], 0.0), writes=[RC])
        kb.op("pool", lambda e: e.memset(ones1[:], 1.0), writes=[RC])
        kb.op("pool", lambda e: e.memset(onesb[:], 1.0 / 1024.0), writes=[RC])
        kb.op("pool", lambda e: e.memset(epsr[:], RMS_EPS), writes=[RC])
        kb.op("pool", lambda e: e.memset(epsl[:], LN_EPS), writes=[RC])
        kb.op("pool", lambda e: e.memset(onesf[:], 1.0), writes=[RC])
        kb.op("pool", lambda e: e.affine_select(
            out=identf[:], in_=onesf[:], pattern=[[-1, 128]], compare_op=ALU.is_equal,
            fill=0.0, base=0, channel_multiplier=1), reads=[RC], writes=[RC])
        kb.op("dve", lambda e: e.tensor_copy(out=identb[:], in_=identf[:]), reads=[RC], writes=[RC])
        S1 = sb("S1", [116, 128])
        S2 = sb("S2", [124, 128])
        RS = kb.res("cstage")
        kb.dma("sp", S1[0:96, :], norm_g[:, :], writes=[RS])
        RS2 = kb.res("cstage2")
        kb.dma("sp", S1[96:104, :], pool_scale[:, :], writes=[RS2])
        RS3 = kb.res("cstage3")
        kb.dma("sp", S1[104:108, :], conv_dw_b[:, :], writes=[RS3])
        RS4 = kb.res("cstage4")
        kb.dma("sp", S1[108:112, :], conv_ln_g[:, :], writes=[RS4])
        RS5 = kb.res("cstage5")
        kb.dma("sp", S1[112:116, :], conv_ln_b[:, :], writes=[RS5])
        RS6 = kb.res("cstage6")
        kb.dma("sp", S2[:, :], conv_dw_w[:, :], writes=[RS6])
        p, rp = next_ps()
        kb.op("pe", lambda e: e.transpose(out=p[:, 0:116], in_=S1[:, :], identity=identf[0:116, 0:116]),
              reads=[RS, RS2, RS3, RS4, RS5, RC], writes=[rp])
        kb.op("pe", lambda e: e.transpose(out=p[:, 128:252], in_=S2[:, :], identity=identf[0:124, 0:124]),
              reads=[RS6, RC], writes=[rp])
        kb.op("dve", lambda e: e.tensor_copy(out=C1[:], in_=p[:, 0:116]), reads=[rp], writes=[RC])
        kb.op("dve", lambda e: e.tensor_copy(out=C2[:], in_=p[:, 128:252]), reads=[rp], writes=[RC])
        for l in range(2):
            for j, i in enumerate((1, 5)):
                src = C1[:, (l * 6 + i) * DC:(l * 6 + i + 1) * DC]
                dst = ghalf[:, (l * 2 + j) * DC:(l * 2 + j + 1) * DC]
                kb.op("dve", lambda e, s=src, d=dst: e.tensor_scalar_mul(out=d, in0=s, scalar1=0.5),
                      reads=[RC], writes=[RC])

    def load_x(es):
        stg = [sb("xst%d" % i, [128, D], es=es) for i in range(4)]
        Rst = [kb.res("xst%d" % i) for i in range(4)]
        for blk in range(17):
            s, rs = stg[blk % 4], Rst[blk % 4]
            if blk < 16:
                n = 128
                kb.dma("sp", s[:, :], x_prompt[blk * 128:(blk + 1) * 128, :], writes=[rs])
                t = blk // 4
                col = blk * 128
            else:
                n = NSAMP
                kb.dma("sp", s[0:n, :], x_sample[:, :], writes=[rs])
                t = 4
                col = SEQ
            for half in range(2):
                p, rp = next_ps()
                for j in range(4):
                    c = half * 4 + j
                    kb.op("pe", lambda e, p=p, s=s, c=c, j=j, n=n: e.transpose(
                        out=p[:, j * 128:j * 128 + n], in_=s[0:n, c * 128:(c + 1) * 128],
                        identity=identf[0:n, 0:n]), reads=[rs, RC], writes=[rp])
                src = p[:, :].rearrange("p (j n) -> p j n", j=4)[:, :, 0:n]
                dst = X[:, half * 4:half * 4 + 4, col:col + n]
                eng = "act" if half == 0 else "dve"
                if eng == "act":
                    kb.op("act", lambda e, d=dst, s_=src: e.copy(out=d, in_=s_), reads=[rp], writes=[RX[t]])
                else:
                    kb.op("dve", lambda e, d=dst, s_=src: e.tensor_copy(out=d, in_=s_), reads=[rp], writes=[RX[t]])

    def store_x(es):
        stg = [sb("yst%d" % i, [128, D], es=es) for i in range(4)]
        Rst = [kb.res("yst%d" % i) for i in range(4)]
        for blk in range(17):
            s, rs = stg[blk % 4], Rst[blk % 4]
            if blk < 16:
                n, t, col = 128, blk // 4, blk * 128
            else:
                n, t, col = NSAMP, 4, SEQ
            for half in range(2):
                p, rp = next_ps()
                for j in range(4):
                    c = half * 4 + j
                    kb.op("pe", lambda e, p=p, c=c, j=j, n=n, col=col: e.transpose(
                        out=p[0:n, j * 128:(j + 1) * 128], in_=X[:, c, col:col + n],
                        identity=identf[:, :]), reads=[RX[t], RC], writes=[rp])
                dst = s[0:n, half * 512:(half + 1) * 512]
                if half == 0:
                    kb.op("act", lambda e, d=dst, p=p, n=n: e.copy(out=d, in_=p[0:n, :]), reads=[rp], writes=[rs])
                else:
                    kb.op("dve", lambda e, d=dst, p=p, n=n: e.tensor_copy(out=d, in_=p[0:n, :]), reads=[rp], writes=[rs])
            if blk < 16:
                kb.dma("sp", y_prompt[blk * 128:(blk + 1) * 128, :], s[:, :], reads=[rs])
            else:
                kb.dma("sp", y_sample[:, :], s[0:n, :], reads=[rs])

    def norm_stats(src_fn, rsrc, n, rstd, rrstd, sqring):
        p, rp = next_ps(0, 5)
        for c in range(DC):
            sq, rsq = sqring[c % len(sqring)]
            kb.op("act", lambda e, sq=sq, c=c: e.activation(out=sq[:, 0:n], in_=src_fn(c), func=AF.Square),
                  reads=[rsrc], writes=[rsq])
            kb.op("pe", lambda e, sq=sq, c=c: e.matmul(p[:, 0:n], lhsT=onesb[:, :], rhs=sq[:, 0:n],
                                                       start=(c == 0), stop=(c == DC - 1)),
                  reads=[rsq, RC], writes=[rp])
        kb.op("act", lambda e: e.activation(out=rstd[:, 0:n], in_=p[:, 0:n], func=AF.Sqrt, bias=epsr[:, 0:1]),
              reads=[rp, RC], writes=[rrstd])
        kb.op("dve", lambda e: e.reciprocal(out=rstd[:, 0:n], in_=rstd[:, 0:n]), reads=[rrstd], writes=[rrstd])

    def ffn_seq(lfs, es):
        alloc_ring(es)
        wdr = [sb("wd%d" % i, [128, FC, 128], BF16, es=es) for i in range(NWD)]
        Rwd = [kb.res("wd%d" % i) for i in range(NWD)]
        H = sb("H", [128, DC, 1040], BF16, es=es)
        AT = sb("AT", [128, FC, 1040], BF16, es=es)
        Y = sb("Y", [128, DC, 1040], es=es)
        RH = [kb.res("Hs%d" % i) for i in range(3)]
        RAT = [kb.res("ATs%d" % i) for i in range(3)]
        RY = [kb.res("Ys%d" % i) for i in range(3)]
        sqs = [(sb("sq%d" % i, [128, 512], BF16, es=es), kb.res("sq%d" % i)) for i in range(3)]
        sgs = [(sb("sg%d" % i, [128, 512], BF16, es=es), kb.res("sg%d" % i)) for i in range(2)]
        rpre = [(sb("rpre%d" % i, [128, 512], es=es), kb.res("rpre%d" % i)) for i in range(3)]
        rpost = [(sb("rpost%d" % i, [128, 512], es=es), kb.res("rpost%d" % i)) for i in range(3)]
        cnt = {"sg": 0, "sq": 0}
        pending = []

        def drain(k):
            for _ in range(k):
                if pending:
                    pending.pop(0)()

        def slots(M):
            loc = {}
            o = 0
            for si, t in enumerate(M):
                loc[t] = (si, slice(o, o + TILES[t][1]))
                o += TILES[t][1]
            return loc

        def prenorm(l, f, M):
            gi_pre = 0 if f == 0 else 4
            loc = slots(M)
            for t in M:
                si, cs = loc[t]
                n = TILES[t][1]
                rstd, rr = rpre[si]
                norm_stats(lambda c, t=t: X[:, c, ts(t)], RX[t], n, rstd, rr, sqs)
                for c in range(DC):
                    kb.op("dve", lambda e, c=c, t=t, rstd=rstd, n=n, cs=cs: e.scalar_tensor_tensor(
                        out=H[:, c, cs], in0=X[:, c, ts(t)], scalar=g_col(l, gi_pre, c),
                        in1=rstd[:, 0:n], op0=ALU.mult, op1=ALU.mult),
                        reads=[RX[t], rr, RC], writes=[RH[si]])

        def phase1(widx, M):
            wgv = ffn_wg[widx].rearrange("(k p) n -> p k n", p=128)
            wuv = ffn_wu[widx].rearrange("(k p) n -> p k n", p=128)
            loc = slots(M)
            for g in range(FC // 2):
                i = ring_ctr["wgu"]
                ring_ctr["wgu"] += 2
                wg, rwg = wgu[i % NWGU], Rwgu[i % NWGU]
                wu, rwu = wgu[(i + 1) % NWGU], Rwgu[(i + 1) % NWGU]
                kb.dma("pool", wg[:, :, :], wgv[:, :, g * 256:(g + 1) * 256], writes=[rwg])
                kb.dma("pool", wu[:, :, :], wuv[:, :, g * 256:(g + 1) * 256], writes=[rwu])
                for t in M:
                    si, cs = loc[t]
                    n = TILES[t][1]
                    for j in range(2):
                        ch = g * 2 + j
                        pg, rpg = next_ps(0, 5)
                        for k in range(DC):
                            kb.op("pe", lambda e, pg=pg, k=k, j=j, n=n, wg=wg, cs=cs: e.matmul(
                                pg[:, 0:n], lhsT=wg[:, k, j * 128:(j + 1) * 128], rhs=H[:, k, cs],
                                start=(k == 0), stop=(k == DC - 1)), reads=[rwg, RH[si]], writes=[rpg])
                        pu, rpu = next_ps(0, 5)
                        for k in range(DC):
                            kb.op("pe", lambda e, pu=pu, k=k, j=j, n=n, wu=wu, cs=cs: e.matmul(
                                pu[:, 0:n], lhsT=wu[:, k, j * 128:(j + 1) * 128], rhs=H[:, k, cs],
                                start=(k == 0), stop=(k == DC - 1)), reads=[rwu, RH[si]], writes=[rpu])
                        sg, rsg = sgs[cnt["sg"] % 2]
                        cnt["sg"] += 1
                        kb.op("act", lambda e, sg=sg, pg=pg, n=n: e.activation(
                            out=sg[:, 0:n], in_=pg[:, 0:n], func=AF.Silu), reads=[rpg], writes=[rsg])
                        kb.op("dve", lambda e, sg=sg, pu=pu, n=n, ch=ch, cs=cs: e.tensor_tensor(
                            out=AT[:, ch, cs], in0=sg[:, 0:n], in1=pu[:, 0:n], op=ALU.mult),
                            reads=[rsg, rpu], writes=[RAT[si]])
                        drain(1)

        def phase2(widx, M):
            wdv = ffn_wd[widx].rearrange("(k p) n -> p k n", p=128)
            loc = slots(M)
            ssb = {t: (PS[7 - loc[t][0]], RPS[7 - loc[t][0]]) for t in M}
            deferred = []
            for c in range(DC):
                i = ring_ctr["wd"]
                ring_ctr["wd"] += 1
                wd, rwd = wdr[i % NWD], Rwd[i % NWD]
                kb.dma("pool", wd[:, :, :], wdv[:, :, c * 128:(c + 1) * 128], writes=[rwd])
                for t in M:
                    si, cs = loc[t]
                    n = TILES[t][1]
                    py, rpy = next_ps(0, 5)
                    for k in range(FC):
                        kb.op("pe", lambda e, py=py, k=k, n=n, wd=wd, cs=cs: e.matmul(
                            py[:, 0:n], lhsT=wd[:, k, :], rhs=AT[:, k, cs],
                            start=(k == 0), stop=(k == FC - 1)), reads=[rwd, RAT[si]], writes=[rpy])
                    while deferred:
                        deferred.pop(0)()
                    kb.op("dve", lambda e, py=py, c=c, n=n, cs=cs: e.tensor_copy(out=Y[:, c, cs], in_=py[:, 0:n]),
                          reads=[rpy], writes=[RY[si]])
                    sq, rsq = sqs[cnt["sq"] % 3]
                    cnt["sq"] += 1
                    kb.op("act", lambda e, sq=sq, c=c, n=n, cs=cs: e.activation(out=sq[:, 0:n], in_=Y[:, c, cs], func=AF.Square),
                          reads=[RY[si]], writes=[rsq])
                    pss, rpss = ssb[t]
                    deferred.append(lambda pss=pss, rpss=rpss, sq=sq, rsq=rsq, n=n, c=c: kb.op(
                        "pe", lambda e: e.matmul(pss[:, 0:n], lhsT=onesb[:, :], rhs=sq[:, 0:n],
                                                 start=(c == 0), stop=(c == DC - 1)), reads=[rsq, RC], writes=[rpss]))
            while deferred:
                deferred.pop(0)()
            return ssb

        def post_pieces(l, f, M, ssb):
            gj = l * 2 + f
            loc = slots(M)
            for t in M:
                si, cs = loc[t]
                n = TILES[t][1]
                rstd, rr = rpost[si]
                pss, rpss = ssb[t]

                def p0(rstd=rstd, rr=rr, pss=pss, rpss=rpss, n=n):
                    kb.op("act", lambda e: e.activation(out=rstd[:, 0:n], in_=pss[:, 0:n], func=AF.Sqrt, bias=epsr[:, 0:1]),
                          reads=[rpss, RC], writes=[rr])
                    kb.op("dve", lambda e: e.reciprocal(out=rstd[:, 0:n], in_=rstd[:, 0:n]), reads=[rr], writes=[rr])
                pending.append(p0)
                for c in range(DC):
                    def pc(c=c, t=t, si=si, cs=cs, rstd=rstd, rr=rr, n=n):
                        kb.op("dve", lambda e: e.scalar_tensor_tensor(
                            out=Y[:, c, cs], in0=Y[:, c, cs], scalar=ghalf[:, gj * DC + c:gj * DC + c + 1],
                            in1=rstd[:, 0:n], op0=ALU.mult, op1=ALU.mult), reads=[rr, RC, RY[si]], writes=[RY[si]])
                        kb.op("dve", lambda e: e.tensor_tensor(
                            out=X[:, c, ts(t)], in0=X[:, c, ts(t)], in1=Y[:, c, cs], op=ALU.add),
                            reads=[RY[si], RX[t]], writes=[RX[t]])
                    pending.append(pc)

        steps = [(l, f, M) for (l, f) in lfs for M in MACROS]
        prenorm(*steps[0])
        for i, (l, f, M) in enumerate(steps):
            widx = l * 2 + f
            phase1(widx, M)
            drain(len(pending))
            if i + 1 < len(steps):
                prenorm(*steps[i + 1])
            ssb = phase2(widx, M)
            post_pieces(l, f, M, ssb)
        drain(len(pending))

    def post_tile(Yt, rY, t, n, gi, l, pss, rpss, rstd, rr):
        kb.op("act", lambda e: e.activation(out=rstd[:, 0:n], in_=pss[:, 0:n], func=AF.Sqrt, bias=epsr[:, 0:1]),
              reads=[rpss, RC], writes=[rr])
        kb.op("dve", lambda e: e.reciprocal(out=rstd[:, 0:n], in_=rstd[:, 0:n]), reads=[rr], writes=[rr])
        for c in range(DC):
            kb.op("dve", lambda e, c=c: e.scalar_tensor_tensor(
                out=Yt[:, c, 0:n], in0=Yt[:, c, 0:n], scalar=g_col(l, gi, c), in1=rstd[:, 0:n],
                op0=ALU.mult, op1=ALU.mult), reads=[rr, RC], writes=[rY])
            kb.op("pool", lambda e, c=c: e.tensor_tensor(
                out=X[:, c, ts(t)], in0=X[:, c, ts(t)], in1=Yt[:, c, 0:n], op=ALU.add),
                reads=[rY, RX[t]], writes=[RX[t]])

    def load_w(dst, rdst, src):
        kb.dma("pool", dst, src, writes=[rdst])


    def gen_sample(es, QT, RQT, KT, RKT, Vs, RVs, LFs, RLFs, attT, Ratt, Rm, triu, tris, onesf2):
        pti = sb("pti", [128, 4 * NPAGES], I32, es)
        idx = sb("idx", [128, 4 * NPAGES], I32, es)
        iop = sb("iop", [128, 1], F32, es)
        Rix = kb.res("idx")
        kb.dma("sp", pti[:, :], page_table.rearrange("b j -> (b j)").partition_broadcast(128), writes=[Rix])
        kb.op("pool", lambda e: e.iota(out=iop[:, :], pattern=[[0, 1]], base=0, channel_multiplier=1,
                                       allow_small_or_imprecise_dtypes=True), writes=[Rix])
        kb.op("dve", lambda e: e.tensor_scalar(out=idx[:, :], in0=pti[:, :], scalar1=128.0, scalar2=iop[:, 0:1],
                                               op0=ALU.mult, op1=ALU.add), reads=[Rix], writes=[Rix])
        Qblk = sb("Qblk", [128, 4, 4, 8], BF16, es)
        RQb = kb.res("Qblk")
        kb.op("dve", lambda e: e.memset(Qblk[:], 0.0), writes=[RQb])
        for pr in range(4):
            kb.op("dve", lambda e, pr=pr: e.tensor_copy(
                out=Qblk[0:64, pr, :, 0:4], in_=QT[0:64, pr, SEQ:SEQ + 16].rearrange("p (b q) -> p b q", b=4)),
                reads=[RQT[4]], writes=[RQb])
            kb.op("dve", lambda e, pr=pr: e.tensor_copy(
                out=Qblk[64:128, pr, :, 4:8], in_=QT[64:128, pr, SEQ:SEQ + 16].rearrange("p (b q) -> p b q", b=4)),
                reads=[RQT[4]], writes=[RQb])
        mask4 = sb("mask4", [4, 4], F32, es)
        negt4 = sb("negt4", [4, 4], F32, es)
        kb.op("pool", lambda e: e.memset(negt4[:], NEG * 8.0), writes=[RQb])
        kb.op("pool", lambda e: e.affine_select(out=mask4[:], in_=negt4[:], pattern=[[-1, 4]], compare_op=ALU.is_gt,
                                                fill=0.0, base=0, channel_multiplier=1), reads=[RQb], writes=[RQb])
        Lb1 = sb("Lb", [128, NPAGES, NH], F32, es)
        RLb = kb.res("Lb")
        Lpg = sb("Lpg", [NPAGES, PAGE * NH], F32, es)
        RLpg = kb.res("Lpg")
        idxp = sb("idxp", [NPAGES, 4], I32, es)
        Ridxp = kb.res("idxp")
        for b_ in range(4):
            kb.dma("sp", idxp[:, b_:b_ + 1], page_table[b_:b_ + 1, :].rearrange("o j -> j o"), writes=[Ridxp])
        clp = cache_logf.rearrange("(n t) h -> n (t h)", t=PAGE)
        Bb = sb("Bb", [128, NPAGES, NH, 1], F32, es)
        RB = kb.res("Bb")
        Ra = sb("Ra", [128, NPAGES + 32, NH], F32, es)
        Rb2 = sb("Rb2", [128, NPAGES + 32, NH], F32, es)
        RRa = kb.res("Ra")
        RRb = kb.res("Rb2")
        kb.op("dve", lambda e: e.memset(Ra[:], 0.0), writes=[RRa])
        kb.op("dve", lambda e: e.memset(Rb2[:], 0.0), writes=[RRb])
        Kp = [(sb("Kp%d" % i, [128, DATT], BF16, es), kb.res("Kp%d" % i)) for i in range(2)]
        Vp = [(sb("Vp%d" % i, [128, DATT], BF16, es), kb.res("Vp%d" % i)) for i in range(4)]
        KTp = [(sb("KTp%d" % i, [128, DATT], BF16, es), kb.res("KTp%d" % i)) for i in range(2)]
        Sb = [(sb("Sb%d" % i, [128, 32], F32, es), kb.res("Sb%d" % i)) for i in range(2)]
        Pb = [(sb("Pb%d" % i, [128, 128], BF16, es), kb.res("Pb%d" % i)) for i in range(2)]
        for i in range(2):
            kb.op("dve", lambda e, i=i: e.memset(Pb[i][0][:], 0.0), writes=[Pb[i][1]])
        biasn = sb("biasn", [4, NH, 4], F32, es)
        negcn = sb("negcn", [4, NH, 1], F32, es)
        Rbn = kb.res("biasn")
        rd = sb("rd", [32, 2], F32, es)
        Pacc = sb("Pacc", [128, 32], F32, es)
        RPacc = kb.res("Pacc")
        On = sb("On", [32, DATT], F32, es)
        ROn = kb.res("On")
        cl = cache_logf
        def logf_gather(bb):
            kb.dma("pool", None, None, fn=lambda e: e.indirect_dma_start(
                out=Lpg[:, :], out_offset=None, in_=clp[:, :],
                in_offset=bass.IndirectOffsetOnAxis(ap=idxp[:, bb:bb + 1], axis=0)),
                reads=[Ridxp], writes=[RLpg], owner=RLpg)

        logf_gather(0)
        for b in range(4):
            plt, rplt = next_ps(5, 7)
            for h in range(NH):
                kb.op("pe", lambda e, h=h: e.transpose(out=plt[:, h * NPAGES:(h + 1) * NPAGES], in_=Lpg[:, h:PAGE * NH:NH],
                                                       identity=identf[0:NPAGES, 0:NPAGES]), reads=[RLpg, RC], writes=[rplt])
            kb.op("dve", lambda e: e.tensor_copy(out=Lb1[:, :, :], in_=plt[:, :].rearrange("p (h j) -> p j h", h=NH)),
                  reads=[rplt], writes=[RLb])
            if b + 1 < 4:
                logf_gather(b + 1)
            RLq = [RLb]
            pw, rpw = next_ps(5, 7)
            ptot, rptot = next_ps(5, 7)
            Lf = Lb1[:, :, :].rearrange("p j h -> p (j h)")
            kb.op("pe", lambda e: e.matmul(pw[:, :], lhsT=tris[:, :], rhs=Lf, start=True, stop=True), reads=RLq + [Rm], writes=[rpw])
            kb.op("pe", lambda e: e.matmul(ptot[:, :], lhsT=onesf2[:, :], rhs=Lf, start=True, stop=True), reads=RLq + [Rm], writes=[rptot])
            kb.op("dve", lambda e: e.tensor_copy(out=Ra[:, 0:NPAGES - 1, :], in_=ptot[:, NH:NPAGES * NH].rearrange("p (j h) -> p j h", h=NH)),
                  reads=[rptot], writes=[RRa])
            cur, rcur, nxt, rnxt = Ra, RRa, Rb2, RRb
            for sh in (1, 2, 4, 8, 16, 32):
                kb.op("dve", lambda e, cur=cur, nxt=nxt, sh=sh: e.tensor_tensor(
                    out=nxt[:, 0:NPAGES, :], in0=cur[:, 0:NPAGES, :], in1=cur[:, sh:NPAGES + sh, :], op=ALU.add),
                    reads=[rcur], writes=[rnxt])
                cur, rcur, nxt, rnxt = nxt, rnxt, cur, rcur
            kb.op("dve", lambda e, cur=cur: e.tensor_tensor(
                out=Bb[:, :, :, 0], in0=pw[:, :].rearrange("p (j h) -> p j h", h=NH), in1=cur[:, 0:NPAGES, :], op=ALU.add),
                reads=[rpw, rcur], writes=[RB])
            kb.op("dve", lambda e: e.memset(Ra[:, NPAGES - 1:, :], 0.0), reads=[RRb], writes=[RRa])
            pcn, rpcn = next_ps(5, 7)
            kb.op("pe", lambda e, b=b: e.matmul(pcn[0:4, 0:NH], lhsT=triu[0:4, 0:4], rhs=LFs[0:4, b, :], start=True, stop=True),
                  reads=[RLFs, Rm], writes=[rpcn])
            kb.op("dve", lambda e: e.tensor_scalar_mul(out=negcn[:, :, 0], in0=pcn[0:4, 0:NH], scalar1=-1.0), reads=[rpcn], writes=[Rbn])
            kb.op("dve", lambda e: e.tensor_tensor(out=biasn[:, :, :], in0=negcn[:, :, :].to_broadcast([4, NH, 4]),
                                                   in1=mask4[:, :].rearrange("s (o t) -> s o t", o=1).to_broadcast([4, NH, 4]),
                                                   op=ALU.add), reads=[Rbn, RQb], writes=[Rbn])
            po, rpo = PS[7], RPS[7]
            kb.op("dve", lambda e: e.memset(Pacc[:], 0.0), writes=[RPacc])
            NKP, NVP = len(Kp), len(Vp)

            def st_gather(j):
                kp, rkp = Kp[j % NKP]
                vp, rvp = Vp[j % NVP]
                off = bass.IndirectOffsetOnAxis(ap=idx[:, b * NPAGES + j:b * NPAGES + j + 1], axis=0)
                kb.dma("pool", None, None, fn=lambda e: e.indirect_dma_start(
                    out=kp[:, :], out_offset=None, in_=cache_k[:, :], in_offset=off), reads=[Rix], writes=[rkp], owner=rkp)
                kb.dma("pool", None, None, fn=lambda e: e.indirect_dma_start(
                    out=vp[:, :], out_offset=None, in_=cache_v[:, :], in_offset=off), reads=[Rix], writes=[rvp], owner=rvp)

            def st_transpose(j):
                kp, rkp = Kp[j % NKP]
                pk, rpk = next_ps(5, 7)
                pkb = pk[:, :].bitcast(BF16)
                for pr in range(4):
                    kb.op("pe", lambda e, pr=pr: e.transpose(
                        out=pkb[:, pr * 128:(pr + 1) * 128], in_=kp[:, pr * 128:(pr + 1) * 128], identity=identb[:, :]),
                        reads=[rkp, RC], writes=[rpk])
                ktp, rktp = KTp[j % 2]
                kb.op("act", lambda e: e.copy(out=ktp[:, :], in_=pkb[:, 0:DATT]), reads=[rpk], writes=[rktp])

            def st_scores(j):
                last = (j == NPAGES)
                ps_, rps_ = next_ps(5, 7)
                sbt, rsb = Sb[j % 2]
                pbt, rpb = Pb[j % 2]
                if not last:
                    ktp, rktp = KTp[j % 2]
                    for pr in range(4):
                        kb.op("pe", lambda e, pr=pr: e.matmul(
                            ps_[:, pr * 8:(pr + 1) * 8], lhsT=ktp[:, pr * 128:(pr + 1) * 128], rhs=Qblk[:, pr, b, :],
                            start=True, stop=True), reads=[rktp, RQb], writes=[rps_])
                    nk = 128
                    bias_ap = Bb[:, j, :, :].to_broadcast([128, NH, 4])
                    rbias = RB
                else:
                    for pr in range(4):
                        kb.op("pe", lambda e, pr=pr: e.matmul(
                            ps_[0:4, pr * 8:(pr + 1) * 8], lhsT=KT[:, pr, SEQ + 4 * b:SEQ + 4 * b + 4], rhs=Qblk[:, pr, b, :],
                            start=True, stop=True), reads=[RKT[4], RQb], writes=[rps_])
                    nk = 4
                    bias_ap = biasn[:, :, :]
                    rbias = Rbn
                kb.op("dve", lambda e: e.scalar_tensor_tensor(
                    out=sbt[0:nk, :].rearrange("p (h q) -> p h q", h=NH), in0=ps_[0:nk, 0:32].rearrange("p (h q) -> p h q", h=NH),
                    scalar=0.125, in1=bias_ap, op0=ALU.mult, op1=ALU.add), reads=[rps_, rbias], writes=[rsb])
                kb.op("act", lambda e: e.activation(out=pbt[0:nk, 0:32], in_=sbt[0:nk, :], func=AF.Exp),
                      reads=[rsb], writes=[rpb])

            def st_pv(j):
                last = (j == NPAGES)
                nk = 4 if last else 128
                pbt, rpb = Pb[j % 2]
                if last:
                    rhs_v, rrv = Vs[0:4, b, :], RVs
                else:
                    vp, rvp = Vp[j % NVP]
                    rhs_v, rrv = vp[:, :], rvp
                kb.op("pe", lambda e: e.matmul(
                    po[:, :], lhsT=pbt[0:nk, :], rhs=rhs_v, start=(j == 0), stop=last), reads=[rpb, rrv], writes=[rpo])
                kb.op("dve", lambda e: e.tensor_tensor(
                    out=Pacc[0:nk, :], in0=Pacc[0:nk, :], in1=pbt[0:nk, 0:32], op=ALU.add), reads=[rpb, RPacc], writes=[RPacc])

            for step in range(NPAGES + 4):
                if step < NPAGES:
                    st_gather(step)
                if 0 <= step - 1 < NPAGES:
                    st_transpose(step - 1)
                if 0 <= step - 2 <= NPAGES:
                    st_scores(step - 2)
                if 0 <= step - 3 <= NPAGES:
                    st_pv(step - 3)
                yield
            pd, rpd = next_ps(5, 7)
            kb.op("pe", lambda e, pd=pd: e.matmul(pd[0:32, 0:2], lhsT=Pacc[:, 0:32], rhs=onesf[:, 0:2], start=True, stop=True),
                  reads=[RPacc, RC], writes=[rpd])
            kb.op("dve", lambda e, pd=pd: e.reciprocal(out=rd[:, :], in_=pd[0:32, 0:2]), reads=[rpd], writes=[ROn])
            kb.op("dve", lambda e: e.tensor_scalar_mul(out=On[:, :], in0=po[0:32, :], scalar1=rd[:, 0:1]), reads=[rpo, ROn], writes=[ROn])
            for pr in range(4):
                ptr, rptr = next_ps(5, 7)
                kb.op("pe", lambda e, ptr=ptr, pr=pr: e.transpose(out=ptr[:, 0:32], in_=On[0:32, pr * 128:(pr + 1) * 128],
                                                                  identity=identf[0:32, 0:32]), reads=[ROn, RC], writes=[rptr])
                for hl in range(2):
                    h = 2 * pr + hl
                    kb.op("dve", lambda e, ptr=ptr, pr=pr, hl=hl, h=h, b=b: e.tensor_copy(
                        out=attT[64 * hl:64 * hl + 64, pr, SEQ + 4 * b:SEQ + 4 * b + 4], in_=ptr[64 * hl:64 * hl + 64, h * 4:h * 4 + 4]),
                        reads=[rptr], writes=[Ratt[4]])


    def gen_conv(es, UT, RUT, UTtail, RUTt, UTs, RUTs, cvT, Rcv, ones512, Rm):
        Yc = sb("Yc", [128, 4, 512], F32, es)
        RYc = kb.res("Yc")
        tmp = sb("ctmp", [128, 512], F32, es)
        Rtmp = kb.res("ctmp")
        mean = sb("cmean", [128, 512], F32, es)
        Rmean = kb.res("cmean")
        crs, Rcrs = tmp, Rtmp
        scst = sb("scst", [30, DCONV], F32, es)
        Rscst = kb.res("scst")
        dg = [(sb("dg%d" % i, [128, 128], BF16, es), kb.res("dg%d" % i)) for i in range(6)]
        PB, RPB = PS[4], RPS[4]
        for b in range(4):
            kb.dma("sp", scst[:, :], state_conv[b], writes=[Rscst])
            for j in range(4):
                kb.op("pe", lambda e, j=j: e.transpose(out=PB[:, j * 32:j * 32 + 30], in_=scst[0:30, j * 128:(j + 1) * 128],
                                                       identity=identf[0:30, 0:30]), reads=[Rscst, RC], writes=[RPB])
            kb.op("dve", lambda e, b=b: e.tensor_copy(
                out=UTs[:, :, b, 0:30], in_=PB[:, 0:128].rearrange("p (j n) -> p j n", j=4)[:, :, 0:30]),
                reads=[RPB], writes=[RUTs])
            rcp = kb.res("cpst%d" % b)
            kb.dma("sp", o_cs[b, 0:26, :], state_conv[b, 4:30, :], writes=[rcp])
            yield

        def ln_swish(nt, dst, rdst):
            for j in range(4):
                kb.op("pe", lambda e, j=j: e.matmul(PB[:, 0:nt], lhsT=ones512[:, :], rhs=Yc[:, j, 0:nt],
                                                    start=(j == 0), stop=(j == 3)), reads=[RYc, Rm], writes=[RPB])
            kb.op("dve", lambda e: e.tensor_copy(out=mean[:, 0:nt], in_=PB[:, 0:nt]), reads=[RPB], writes=[Rmean])
            for j in range(4):
                kb.op("act", lambda e, j=j: e.activation(out=tmp[:, 0:nt], in_=Yc[:, j, 0:nt], func=AF.Square),
                      reads=[RYc], writes=[Rtmp])
                kb.op("pe", lambda e, j=j: e.matmul(PB[:, 0:nt], lhsT=ones512[:, :], rhs=tmp[:, 0:nt],
                                                    start=(j == 0), stop=(j == 3)), reads=[Rtmp, Rm], writes=[RPB])
            kb.op("dve", lambda e: e.tensor_tensor(out=crs[:, 0:nt], in0=mean[:, 0:nt], in1=mean[:, 0:nt], op=ALU.mult),
                  reads=[Rmean], writes=[Rcrs])
            kb.op("dve", lambda e: e.tensor_tensor(out=crs[:, 0:nt], in0=PB[:, 0:nt], in1=crs[:, 0:nt], op=ALU.subtract),
                  reads=[RPB, Rcrs], writes=[Rcrs])
            kb.op("act", lambda e: e.activation(out=crs[:, 0:nt], in_=crs[:, 0:nt], func=AF.Sqrt, bias=epsl[:, 0:1]),
                  reads=[Rcrs, RC], writes=[Rcrs])
            kb.op("dve", lambda e: e.reciprocal(out=crs[:, 0:nt], in_=crs[:, 0:nt]), reads=[Rcrs], writes=[Rcrs])
            for j in range(4):
                kb.op("dve", lambda e, j=j: e.tensor_tensor(out=Yc[:, j, 0:nt], in0=Yc[:, j, 0:nt], in1=mean[:, 0:nt], op=ALU.subtract),
                      reads=[Rmean, RYc], writes=[RYc])
                kb.op("dve", lambda e, j=j: e.tensor_tensor(out=Yc[:, j, 0:nt], in0=Yc[:, j, 0:nt], in1=crs[:, 0:nt], op=ALU.mult),
                      reads=[Rcrs, RYc], writes=[RYc])
                kb.op("act", lambda e, j=j: e.activation(out=dst(j), in_=Yc[:, j, 0:nt], func=AF.Silu,
                                                         scale=C1[:, 108 + j:109 + j], bias=C1[:, 112 + j:113 + j]),
                      reads=[RYc, RC], writes=[rdst])

        ndg = 0
        for t in range(4):
            for j in range(4):
                for tap in range(CONVW):
                    d_, rd_ = dg[ndg % 6]
                    ndg += 1
                    kb.op("dve", lambda e, d_=d_, tap=tap, j=j: e.tensor_scalar_mul(
                        out=d_[:, :], in0=identb[:, :], scalar1=C2[:, tap * 4 + j:tap * 4 + j + 1]), reads=[RC], writes=[rd_])
                    kb.op("pe", lambda e, d_=d_, tap=tap, j=j, t=t: e.matmul(
                        PB[:, :], lhsT=d_[:, :], rhs=UT[:, j, t * 512 + tap:t * 512 + tap + 512],
                        start=(tap == 0), stop=(tap == CONVW - 1)),
                        reads=[rd_, RUT[t]] + ([RUT[t - 1]] if t > 0 else []), writes=[RPB])
                    if tap % 8 == 7:
                        yield
                kb.op("act", lambda e, j=j: e.activation(out=Yc[:, j, :], in_=PB[:, :], func=AF.Identity,
                                                         bias=C1[:, 104 + j:105 + j]), reads=[RPB, RC], writes=[RYc])
                yield
            ln_swish(512, lambda j, t=t: cvT[:, j, ts(t)], Rcv[t])
            yield
        for j in range(4):
            kb.op("dve", lambda e, j=j: e.tensor_scalar(
                out=Yc[:, j, 0:16].rearrange("p (b q) -> p b q", b=4), in0=UTs[:, j, :, 0:4], scalar1=C2[:, j:j + 1],
                scalar2=C1[:, 104 + j:105 + j], op0=ALU.mult, op1=ALU.add), reads=[RUTs, RC], writes=[RYc])
            for tap in range(1, CONVW):
                kb.op("dve", lambda e, j=j, tap=tap: e.scalar_tensor_tensor(
                    out=Yc[:, j, 0:16].rearrange("p (b q) -> p b q", b=4), in0=UTs[:, j, :, tap:tap + 4],
                    scalar=C2[:, tap * 4 + j:tap * 4 + j + 1], in1=Yc[:, j, 0:16].rearrange("p (b q) -> p b q", b=4),
                    op0=ALU.mult, op1=ALU.add), reads=[RUTs, RC, RYc], writes=[RYc])
            yield
        ln_swish(16, lambda j: cvT[:, j, SEQ:SEQ + 16], Rcv[4])
        yield
        ost, Rost = scst, Rscst
        for b in range(5):
            for j in range(4):
                src = UTtail[:, j, :] if b == 4 else UTs[:, j, b, 30:34]
                nn = 30 if b == 4 else 4
                kb.op("pe", lambda e, j=j, src=src, nn=nn: e.transpose(
                    out=PB[0:nn, j * 128:(j + 1) * 128], in_=src, identity=identf[:, :]),
                    reads=[RUTt if b == 4 else RUTs, RC], writes=[RPB])
            kb.op("dve", lambda e, nn=nn: e.tensor_copy(out=ost[0:nn, :], in_=PB[0:nn, :]), reads=[RPB], writes=[Rost])
            if b == 4:
                kb.dma("sp", o_cp[:, :], ost[0:30, :], reads=[Rost])
            else:
                kb.dma("sp", o_cs[b, 26:30, :], ost[0:4, :], reads=[Rost])
            yield

    def gen_prompt(esc, QT, RQT, KT, RKT, Vx, RV, negc, Rnegc, maskneg, Rm, attT, Ratt):
        PTs = [(sb("PT%d" % i, [128, 512], BF16, esc), kb.res("PT%d" % i)) for i in range(2)]
        Qz = [(sb("Qz%d" % i, [128, 512], BF16, esc), kb.res("Qz%d" % i)) for i in range(2)]
        att_tok = sb("att_tok", [128, 4, 128], BF16, esc)
        Rat = kb.res("att_tok")
        rdn = sb("rdn", [128, 8, 1], F32, esc)
        Rrdn = kb.res("rdn")
        for hl in range(2):
            kb.op("dve", lambda e, hl=hl: e.memset(Qz[hl][0][:], 0.0), writes=[Qz[hl][1]])
        npt = 0
        for pr in range(4):
            for qt in range(4):
                q0 = qt * 512
                for hl in range(2):
                    rows = slice(64 * hl, 64 * hl + 64)
                    kb.op("dve", lambda e, hl=hl, rows=rows: e.tensor_copy(out=Qz[hl][0][rows, :], in_=QT[rows, pr, q0:q0 + 512]),
                          reads=[RQT[qt]], writes=[Qz[hl][1]])
                pO = [(PS[2], RPS[2]), (PS[3], RPS[3])]
                pOv = [p[0][:, 0:264].rearrange("p (q c) -> p q c", c=66) for p in pO]
                first = [True, True]
                nkt = 4 * qt + 4
                steps = [(kt, hl) for kt in range(nkt) for hl in range(2)]

                def emit_s(kt, hl):
                    nonlocal npt
                    jd = kt - 4 * qt
                    qs = max(jd, 0) * 128
                    n = 512 - qs
                    h = 2 * pr + hl
                    ps_, rps_ = next_ps(0, 2)
                    kb.op("pe", lambda e: e.matmul(
                        ps_[:, 0:n], lhsT=KT[:, pr, kt * 128:(kt + 1) * 128], rhs=Qz[hl][0][:, qs:512],
                        start=True, stop=(jd < 0)), reads=[RKT[kt // 4], Qz[hl][1]], writes=[rps_])
                    if jd >= 0:
                        kb.op("pe", lambda e: e.matmul(ps_[:, 0:128], lhsT=identb[:, :], rhs=maskneg[:, :],
                                                       start=False, stop=True), reads=[Rm, RC], writes=[rps_])
                    pt, rpt = PTs[npt % 2]
                    npt += 1
                    kb.op("act", lambda e: e.activation(
                        out=pt[:, 0:n], in_=ps_[:, 0:n], func=AF.Exp, scale=0.125, bias=negc[:, kt, h:h + 1]),
                        reads=[rps_, Rnegc], writes=[rpt])
                    return (kt, hl, jd, qs, h, pt, rpt)

                def emit_pv(info):
                    kt, hl, jd, qs, h, pt, rpt = info
                    for qb in range(max(jd, 0), 4):
                        lo = qb * 128 - qs
                        st_ = first[hl]
                        first[hl] = False
                        kb.op("pe", lambda e, qb=qb, lo=lo, st_=st_: e.matmul(
                            pOv[hl][:, qb, :], lhsT=pt[:, lo:lo + 128], rhs=Vx[:, kt, h, :],
                            start=st_, stop=(kt == 4 * qt + qb), skip_group_check=True),
                            reads=[rpt, RV[kt]], writes=[pO[hl][1]])

                prev = emit_s(*steps[0])
                for si in range(1, len(steps) + 1):
                    cur = emit_s(*steps[si]) if si < len(steps) else None
                    emit_pv(prev)
                    prev = cur
                    if si % 2 == 0:
                        yield
                for hl in range(2):
                    kb.op("dve", lambda e, hl=hl: e.reciprocal(out=rdn[:, hl * 4:(hl + 1) * 4, :], in_=pOv[hl][:, :, 64:65]),
                          reads=[pO[hl][1]], writes=[Rrdn])
                    kb.op("dve", lambda e, hl=hl: e.tensor_tensor(
                        out=att_tok[:, :, hl * 64:(hl + 1) * 64], in0=pOv[hl][:, :, 0:64],
                        in1=rdn[:, hl * 4:(hl + 1) * 4, :].to_broadcast([128, 4, 64]), op=ALU.mult),
                        reads=[pO[hl][1], Rrdn], writes=[Rat])
                ptb, rptb = next_ps(0, 2)
                ptbb = ptb[:, :].bitcast(BF16)
                for qb in range(4):
                    kb.op("pe", lambda e, qb=qb, ptbb=ptbb: e.transpose(out=ptbb[:, qb * 128:(qb + 1) * 128], in_=att_tok[:, qb, :],
                                                                       identity=identb[:, :]), reads=[Rat, RC], writes=[rptb])
                kb.op("act", lambda e, ptbb=ptbb, pr=pr, q0=q0: e.copy(out=attT[:, pr, q0:q0 + 512], in_=ptbb[:, 0:512]),
                      reads=[rptb], writes=[Ratt[qt]])
                yield

    def mixer0(es):
        l = 0
        QT = sb("QT", [128, 4, NTOK], BF16, es)
        KT = sb("KT", [128, 4, NTOK], BF16, es)
        RQT = [kb.res("QT%d" % t) for t in range(5)]
        RKT = [kb.res("KT%d" % t) for t in range(5)]
        Vx = sb("Vx", [128, 16, NH, 66], BF16, es)
        RV = [kb.res("V%d" % b) for b in range(16)]
        kb.op("pool", lambda e: e.memset(Vx[:, :, :, 64:65], 1.0), writes=RV)
        kb.op("pool", lambda e: e.memset(Vx[:, :, :, 65:66], 0.0), writes=RV)
        Vs = sb("Vs", [4, 4, DATT], BF16, es)
        RVs = kb.res("Vs")
        cvT = sb("cvT", [128, 4, NTOK], BF16, es)
        Rcv = [kb.res("cv%d" % t) for t in range(5)]
        attT = QT
        Ratt = RQT
        LF = sb("LF", [128, 16, NH], F32, es)
        RLF = kb.res("LF")
        LFs = sb("LFs", [4, 4, NH], F32, es)
        RLFs = kb.res("LFs")
        negc = sb("negc", [128, 16, NH], F32, es)
        Rnegc = kb.res("negc")
        fgb = sb("fgb", [128, NH], F32, es)
        Rm = kb.res("mixc")
        kb.dma("sp", fgb[:, :], fgate_b.partition_broadcast(128), writes=[Rm])
        triu = sb("triu", [128, 128], F32, es)
        tris = sb("tris", [128, 128], F32, es)
        onesf2 = onesf
        ones512 = sb("ones512", [128, 128], F32, es)
        maskneg = sb("maskneg", [128, 128], BF16, es)
        negt = ones512
        kb.op("pool", lambda e: e.memset(negt[:], NEG), reads=[RC], writes=[Rm])
        kb.op("pool", lambda e: e.affine_select(out=triu[:], in_=onesf2[:], pattern=[[1, 128]], compare_op=ALU.is_ge,
                                                fill=0.0, base=0, channel_multiplier=-1), reads=[Rm], writes=[Rm])
        kb.op("pool", lambda e: e.affine_select(out=tris[:], in_=onesf2[:], pattern=[[-1, 128]], compare_op=ALU.is_gt,
                                                fill=0.0, base=0, channel_multiplier=1), reads=[Rm], writes=[Rm])
        kb.op("pool", lambda e: e.affine_select(out=maskneg[:], in_=negt[:], pattern=[[-1, 128]], compare_op=ALU.is_gt,
                                                fill=0.0, base=0, channel_multiplier=1), reads=[Rm], writes=[Rm])
        kb.op("pool", lambda e: e.memset(ones512[:], 1.0 / 512.0), reads=[Rm], writes=[Rm])
        win = mix_w_in.rearrange("(k p) n -> p k n", p=128)

        with ExitStack() as esu:
            UT = sb("UT", [128, 4, 30 + SEQ], BF16, esu)
            UTtail = sb("UTtail", [128, 4, 30], F32, esu)
            RUTt = kb.res("UTtail")
            RUT = [kb.res("UT%d" % t) for t in range(4)]
            UTs = sb("UTs", [128, 4, 4, 34], F32, esu)
            RUTs = kb.res("UTs")
            kb.op("pool", lambda e: e.memset(UT[:, :, 0:30], 0.0), writes=[RUT[0]])
            with ExitStack() as esa:
                alloc_ring(esa)
                Hb = [sb("Hm%d" % i, [128, DC, 512], BF16, esa) for i in range(2)]
                RHb = [kb.res("Hmb%d" % i) for i in range(2)]
                stg = [(sb("kvst%d" % i, [128, DATT], F32, esa), kb.res("kvst%d" % i)) for i in range(2)]
                lst = [(sb("lfst%d" % i, [128, 2 * NH], F32, esa), kb.res("lfst%d" % i)) for i in range(2)]
                sqs = [(sb("msq%d" % i, [128, 512], BF16, esa), kb.res("msq%d" % i)) for i in range(2)]
                sig = [(sb("sig%d" % i, [128, 512], F32, esa), kb.res("sig%d" % i)) for i in range(1)]
                rstd, rr = sb("mrstd", [128, 512], F32, esa), kb.res("mrstd")
                cnt = {"s": 0, "l": 0, "g": 0}
                def emit_norm(t):
                    n = TILES[t][1]
                    Hn, rHn = Hb[t % 2], RHb[t % 2]
                    norm_stats(lambda c, t=t: X[:, c, ts(t)], RX[t], n, rstd, rr, sqs)
                    for c in range(DC):
                        kb.op("dve", lambda e, c=c, t=t, n=n: e.scalar_tensor_tensor(
                            out=Hn[:, c, 0:n], in0=X[:, c, ts(t)], scalar=g_col(l, 2, c),
                            in1=rstd[:, 0:n], op0=ALU.mult, op1=ALU.mult),
                            reads=[RX[t], rr, RC], writes=[rHn])

                emit_norm(0)
                for mi, M in enumerate([[0], [1], [2], [3], [4]]):
                    H = Hb[mi % 2]
                    RH = {t: RHb[mi % 2] for t in M}
                    loc = {}
                    o = 0
                    for t in M:
                        loc[t] = slice(o, o + TILES[t][1])
                        o += TILES[t][1]
                    blocks = []
                    for t in M:
                        if t < 4:
                            for bb in range(4):
                                blocks.append((t, loc[t].start + bb * 128, 128, "p", t * 4 + bb))
                        else:
                            for b in range(4):
                                blocks.append((t, loc[t].start + b * 4, 4, "s", b))

                    def feat_proj(col0, dst, rdst):
                        s0, s1 = ring_next(), ring_next()
                        load_w(wgu[s0][:, :, :], Rwgu[s0], win[:, :, col0:col0 + 256])
                        load_w(wgu[s1][:, :, :], Rwgu[s1], win[:, :, col0 + 256:col0 + 512])
                        sls = (s0, s1)
                        for t in M:
                            n = TILES[t][1]
                            for j in range(4):
                                w, rw = wgu[sls[j // 2]], Rwgu[sls[j // 2]]
                                p, rp = next_ps(0, 6)
                                for k in range(DC):
                                    kb.op("pe", lambda e, p=p, k=k, j=j, w=w, t=t, n=n: e.matmul(
                                        p[:, 0:n], lhsT=w[:, k, (j % 2) * 128:(j % 2) * 128 + 128], rhs=H[:, k, loc[t]],
                                        start=(k == 0), stop=(k == DC - 1)), reads=[rw, RH[t]], writes=[rp])
                                kb.op("act", lambda e, p=p, j=j, t=t, n=n: e.copy(out=dst[:, j, ts(t)], in_=p[:, 0:n]),
                                      reads=[rp], writes=[rdst[t]])
                        return sls

                    def tok_proj(col0, which, sl=None):
                        if sl is None:
                            sl = (ring_next(), ring_next())
                            load_w(wgu[sl[0]][:, :, :], Rwgu[sl[0]], win[:, :, col0:col0 + 256])
                            load_w(wgu[sl[1]][:, :, :], Rwgu[sl[1]], win[:, :, col0 + 256:col0 + 512])
                        for (t, c0, nb, kind, idx) in blocks:
                            p, rp = next_ps(0, 6)
                            for hf in range(2):
                                w, rw = wgu[sl[hf]], Rwgu[sl[hf]]
                                for k in range(DC):
                                    kb.op("pe", lambda e, p=p, k=k, w=w, c0=c0, nb=nb, hf=hf: e.matmul(
                                        p[0:nb, hf * 256:(hf + 1) * 256], lhsT=H[:, k, c0:c0 + nb], rhs=w[:, k, :],
                                        start=(k == 0), stop=(k == DC - 1)), reads=[rw, RH[t]], writes=[rp])
                            st, rst = stg[cnt["s"] % 2]
                            cnt["s"] += 1
                            kb.op("dve", lambda e, st=st, p=p, nb=nb: e.tensor_copy(out=st[0:nb, :], in_=p[0:nb, :]),
                                  reads=[rp], writes=[rst])
                            if which == "v":
                                if kind == "p":
                                    kb.op("act", lambda e, st=st, idx=idx: e.copy(
                                        out=Vx[:, idx, :, 0:64], in_=st[:, :].rearrange("p (h d) -> p h d", h=NH)),
                                        reads=[rst], writes=[RV[idx]])
                                else:
                                    kb.op("act", lambda e, st=st, idx=idx: e.copy(out=Vs[0:4, idx, :], in_=st[0:4, :]),
                                          reads=[rst], writes=[RVs])
                            if kind == "p":
                                dst = (o_kp if which == "k" else o_vp)[idx * 128:(idx + 1) * 128, :]
                            else:
                                dst = (o_ks if which == "k" else o_vs)[idx * 4:(idx + 1) * 4, :]
                            kb.dma("sp", dst, st[0:nb, :], reads=[rst])

                    feat_proj(0, QT, RQT)
                    if mi + 1 < 5:
                        emit_norm(mi + 1)
                    ksl = feat_proj(512, KT, RKT)
                    tok_proj(512, "k", ksl)
                    tok_proj(1024, "v")
                    fs = ring_next()
                    load_w(wgu[fs][:, :, 0:NH], Rwgu[fs], win[:, :, 1536:1536 + NH])
                    for (t, c0, nb, kind, idx) in blocks:
                        p, rp = next_ps(0, 6)
                        for k in range(DC):
                            kb.op("pe", lambda e, p=p, k=k, c0=c0, nb=nb: e.matmul(
                                p[0:nb, 0:NH], lhsT=H[:, k, c0:c0 + nb], rhs=wgu[fs][:, k, 0:NH],
                                start=(k == 0), stop=(k == DC - 1)), reads=[Rwgu[fs], RH[t]], writes=[rp])
                        st, rst = lst[cnt["l"] % 2]
                        cnt["l"] += 1
                        kb.op("dve", lambda e, st=st, p=p, nb=nb: e.tensor_tensor(
                            out=st[0:nb, 0:NH], in0=p[0:nb, 0:NH], in1=fgb[0:nb, :], op=ALU.add),
                            reads=[rp, Rm], writes=[rst])
                        kb.op("act", lambda e, st=st, nb=nb: e.activation(out=st[0:nb, 0:NH], in_=st[0:nb, 0:NH],
                                                                        func=AF.Exp, scale=-1.0), reads=[rst], writes=[rst])
                        kb.op("act", lambda e, st=st, nb=nb: e.activation(out=st[0:nb, 0:NH], in_=st[0:nb, 0:NH],
                                                                        func=AF.Ln, bias=1.0), reads=[rst], writes=[rst])
                        if kind == "p":
                            kb.op("dve", lambda e, st=st, idx=idx: e.tensor_scalar_mul(out=LF[:, idx, :], in0=st[:, 0:NH], scalar1=-1.0),
                                  reads=[rst], writes=[RLF])
                            kb.op("dve", lambda e, st=st, idx=idx: e.tensor_copy(out=st[:, NH:2 * NH], in_=LF[:, idx, :]),
                                  reads=[RLF], writes=[rst])
                            kb.dma("sp", o_lp[idx * 128:(idx + 1) * 128, :], st[:, NH:2 * NH], reads=[rst])
                        else:
                            kb.op("dve", lambda e, st=st, idx=idx: e.tensor_scalar_mul(out=LFs[0:4, idx, :], in0=st[0:4, 0:NH], scalar1=-1.0),
                                  reads=[rst], writes=[RLFs])
                            kb.op("dve", lambda e, st=st, idx=idx: e.tensor_copy(out=st[0:4, NH:2 * NH], in_=LFs[0:4, idx, :]),
                                  reads=[RLFs], writes=[rst])
                            kb.dma("sp", o_ls[idx * 4:(idx + 1) * 4, :], st[0:4, NH:2 * NH], reads=[rst])
                    gsl = [ring_next() for _ in range(4)]
                    for i in range(2):
                        load_w(wgu[gsl[i]][:, :, :], Rwgu[gsl[i]], win[:, :, 1544 + i * 256:1544 + (i + 1) * 256])
                        load_w(wgu[gsl[2 + i]][:, :, :], Rwgu[gsl[2 + i]], win[:, :, 2056 + i * 256:2056 + (i + 1) * 256])
                    for t in M:
                        n = TILES[t][1]
                        for j in range(4):
                            pa, rpa = next_ps(0, 6)
                            pg, rpg = next_ps(0, 6)
                            for (p, rp, base) in ((pa, rpa, 0), (pg, rpg, 2)):
                                w, rw = wgu[gsl[base + j // 2]], Rwgu[gsl[base + j // 2]]
                                for k in range(DC):
                                    kb.op("pe", lambda e, p=p, k=k, j=j, w=w, t=t, n=n: e.matmul(
                                        p[:, 0:n], lhsT=w[:, k, (j % 2) * 128:(j % 2) * 128 + 128], rhs=H[:, k, loc[t]],
                                        start=(k == 0), stop=(k == DC - 1)), reads=[rw, RH[t]], writes=[rp])
                            sg, rsg = sig[0]
                            cnt["g"] += 1
                            kb.op("act", lambda e, sg=sg, pg=pg, n=n: e.activation(out=sg[:, 0:n], in_=pg[:, 0:n], func=AF.Sigmoid),
                                  reads=[rpg], writes=[rsg])
                            if t < 4:
                                kb.op("dve", lambda e, sg=sg, pa=pa, j=j, t=t: e.tensor_tensor(
                                    out=UT[:, j, 30 + t * 512:30 + (t + 1) * 512], in0=pa[:, 0:512], in1=sg[:, 0:512], op=ALU.mult),
                                    reads=[rpa, rsg], writes=[RUT[t]])
                                if t == 3:
                                    kb.op("dve", lambda e, sg=sg, pa=pa, j=j: e.tensor_tensor(
                                        out=UTtail[:, j, :], in0=pa[:, 482:512], in1=sg[:, 482:512], op=ALU.mult),
                                        reads=[rpa, rsg], writes=[RUTt])
                            else:
                                kb.op("dve", lambda e, sg=sg, pa=pa, j=j: e.tensor_tensor(
                                    out=UTs[:, j, :, 30:34], in0=pa[:, 0:16].rearrange("p (b q) -> p b q", b=4),
                                    in1=sg[:, 0:16].rearrange("p (b q) -> p b q", b=4), op=ALU.mult),
                                    reads=[rpa, rsg], writes=[RUTs])
                lacc = sb("lacc", [128, NH], F32, esa)
                Rla = kb.res("lacc")
                kb.op("dve", lambda e: e.memset(lacc[:], 0.0), writes=[Rla])
                for blk in range(16):
                    p, rp = next_ps(0, 6)
                    kb.op("pe", lambda e, p=p, blk=blk: e.matmul(p[:, 0:NH], lhsT=triu[:, :], rhs=LF[:, blk, :], start=True, stop=False),
                          reads=[RLF, Rm], writes=[rp])
                    kb.op("pe", lambda e, p=p: e.matmul(p[:, 0:NH], lhsT=onesf2[:, :], rhs=lacc[:, :], start=False, stop=True),
                          reads=[Rla, Rm], writes=[rp])
                    kb.op("dve", lambda e, p=p, blk=blk: e.tensor_scalar_mul(out=negc[:, blk, :], in0=p[:, 0:NH], scalar1=-1.0),
                          reads=[rp], writes=[Rnegc])
                    kb.op("dve", lambda e, blk=blk: e.tensor_tensor(out=lacc[:], in0=lacc[:], in1=LF[:, blk, :], op=ALU.add),
                          reads=[RLF, Rla], writes=[Rla])
                kb.barrier()
            with ExitStack() as esg:
                gens = [gen_prompt(esg, QT, RQT, KT, RKT, Vx, RV, negc, Rnegc, maskneg, Rm, attT, Ratt),
                        gen_sample(esg, QT, RQT, KT, RKT, Vs, RVs, LFs, RLFs, attT, Ratt, Rm, triu, tris, onesf2),
                        gen_conv(esg, UT, RUT, UTtail, RUTt, UTs, RUTs, cvT, Rcv, ones512, Rm)]
                weights = [1, 1, 1]
                alive = [True, True, True]
                rnd = 0
                while any(alive):
                    for gi_, g in enumerate(gens):
                        if not alive[gi_]:
                            continue
                        if gi_ == 2 and rnd % 3 != 0:
                            continue
                        for _ in range(weights[gi_]):
                            try:
                                next(g)
                            except StopIteration:
                                alive[gi_] = False
                                break
                    rnd += 1
                kb.barrier()
        with ExitStack() as esd:
            alloc_ring(esd)
            Yts = [sb("Ymix%d" % i, [128, DC, 512], F32, esd) for i in range(2)]
            RYts = [kb.res("Ymix%d" % i) for i in range(2)]
            sqs = [(sb("dsq%d" % i, [128, 512], BF16, esd), kb.res("dsq%d" % i)) for i in range(3)]
            rstds = [(sb("drstd%d" % i, [128, 512], F32, esd), kb.res("drstd%d" % i)) for i in range(2)]
            wo = mix_w_out.rearrange("(k p) n -> p k n", p=128)
            for i in range(4):
                load_w(wgu[i][:, :, :], Rwgu[i], wo[:, :, i * 256:(i + 1) * 256])
            nsq = [0]

            def d_mm(t):
                n = TILES[t][1]
                Yt, RYt = Yts[t % 2], RYts[t % 2]
                pss, rpss = PS[7 - t % 2], RPS[7 - t % 2]
                deferred = []
                for c in range(DC):
                    w, rw = wgu[c // 2], Rwgu[c // 2]
                    p, rp = next_ps(0, 6)
                    for k in range(DC):
                        src, rsrc = (attT, Ratt) if k < 4 else (cvT, Rcv)
                        kb.op("pe", lambda e, p=p, k=k, c=c, w=w, src=src: e.matmul(
                            p[:, 0:n], lhsT=w[:, k, (c % 2) * 128:(c % 2) * 128 + 128], rhs=src[:, k % 4, ts(t)],
                            start=(k == 0), stop=(k == DC - 1)), reads=[rw, rsrc[t]], writes=[rp])
                    while deferred:
                        deferred.pop(0)()
                    kb.op("dve", lambda e, p=p, c=c: e.tensor_copy(out=Yt[:, c, 0:n], in_=p[:, 0:n]), reads=[rp], writes=[RYt])
                    sq, rsq = sqs[nsq[0] % 3]
                    nsq[0] += 1
                    kb.op("act", lambda e, sq=sq, c=c: e.activation(out=sq[:, 0:n], in_=Yt[:, c, 0:n], func=AF.Square),
                          reads=[RYt], writes=[rsq])
                    deferred.append(lambda sq=sq, rsq=rsq, c=c: kb.op(
                        "pe", lambda e: e.matmul(pss[:, 0:n], lhsT=onesb[:, :], rhs=sq[:, 0:n],
                                                 start=(c == 0), stop=(c == DC - 1)), reads=[rsq, RC], writes=[rpss]))
                while deferred:
                    deferred.pop(0)()

            def d_post(t):
                post_tile(Yts[t % 2], RYts[t % 2], t, TILES[t][1], 3, l, PS[7 - t % 2], RPS[7 - t % 2],
                          rstds[t % 2][0], rstds[t % 2][1])

            d_mm(0)
            for t in range(1, 5):
                d_mm(t)
                d_post(t - 1)
            d_post(4)

    def mixer1(es):
        l = 1
        Rm = kb.res("m1c")
        invc = sb("invc", [128, 15], F32, es)
        for pos in range(15):
            kb.op("pool", lambda e, pos=pos: e.memset(invc[:, pos:pos + 1], 1.0 / (pos + 1)), writes=[Rm])
        alloc_ring(es)
        for gi in range(4):
            load_w(wgu[0][:, 2 * gi:2 * gi + 2, :], Rwgu[0], pool_w[gi].rearrange("(ci p) n -> p ci n", p=128))
        PW, RPW = wgu[0], Rwgu[0]
        L = 15 + 512
        HFs = [sb("HF%d" % i, [128, DC, L], F32, es) for i in range(2)]
        RHF = [kb.res("HF%d" % i) for i in range(2)]
        A = sb("WA", [128, DC, L], F32, es)
        B = sb("WB", [128, 6, L], F32, es)
        RA, RB_ = kb.res("WA"), kb.res("WB")
        MXs = [sb("MX%d" % i, [128, DC, 512], BF16, es) for i in range(2)]
        RMXs = [kb.res("MX%d" % i) for i in range(2)]
        Yts = [sb("Ypool%d" % i, [128, DC, 512], F32, es) for i in range(2)]
        RYts = [kb.res("Ypool%d" % i) for i in range(2)]
        rposts = [(sb("prpost%d" % i, [128, 512], F32, es), kb.res("prpost%d" % i)) for i in range(2)]
        sqs = [(sb("psq%d" % i, [128, 512], BF16, es), kb.res("psq%d" % i)) for i in range(2)]
        rstd, rr = sb("prstd", [128, 512], F32, es), kb.res("prstd")
        stg = sb("pstg", [15, D], F32, es)
        Rstg = kb.res("pstg")
        nsq = [0]

        def window_mix(HF, rHF, G, Lg, first, par):
            MX, RMX = MXs[par], RMXs[par]
            n = Lg - 15
            A4 = A[:, :, 0:G * Lg].rearrange("p c (g l) -> p c g l", g=G)
            B4 = B[:, :, 0:G * Lg].rearrange("p c (g l) -> p c g l", g=G)
            kb.op("dve", lambda e: e.tensor_tensor(out=A4[:, :, :, 1:Lg], in0=HF[:, :, :, 1:Lg], in1=HF[:, :, :, 0:Lg - 1], op=ALU.add),
                  reads=[rHF], writes=[RA])
            kb.op("dve", lambda e: e.tensor_tensor(out=B4[:, 0:6, :, 3:Lg], in0=A4[:, 2:8, :, 3:Lg], in1=A4[:, 2:8, :, 1:Lg - 2], op=ALU.add),
                  reads=[RA], writes=[RB_])
            kb.op("dve", lambda e: e.tensor_tensor(out=A4[:, 4:8, :, 7:Lg], in0=B4[:, 2:6, :, 7:Lg], in1=B4[:, 2:6, :, 3:Lg - 4], op=ALU.add),
                  reads=[RB_, RA], writes=[RA])
            kb.op("dve", lambda e: e.tensor_tensor(out=B4[:, 4:6, :, 15:Lg], in0=A4[:, 6:8, :, 15:Lg], in1=A4[:, 6:8, :, 7:Lg - 8], op=ALU.add),
                  reads=[RA, RB_], writes=[RB_])
            for c in range(DC):
                gi = c // 2
                w = 2 << gi
                src = A4[:, c] if gi in (0, 2) else B4[:, c - 2]
                mx = MX[:, c, 0:G * n].rearrange("p (g n) -> p g n", g=G)
                kb.op("dve", lambda e, src=src, mx=mx, c=c, w=w: e.scalar_tensor_tensor(
                    out=mx, in0=src[:, :, 15:Lg], scalar=1.0 / w, in1=HF[:, c, :, 15:Lg], op0=ALU.mult, op1=ALU.subtract),
                    reads=[RA, RB_, rHF], writes=[RMX])
                if first and w > 1:
                    kb.op("dve", lambda e, src=src, c=c, w=w: e.tensor_tensor(
                        out=A4[:, c, :, 0:w - 1] if gi in (1, 3) else B4[:, 0, :, 0:w - 1], in0=src[:, :, 15:15 + w - 1],
                        in1=invc[:, 0:w - 1].rearrange("p (g n) -> p g n", g=1), op=ALU.mult), reads=[RA, RB_, Rm], writes=[RA, RB_])
                    tmpv = A4[:, c, :, 0:w - 1] if gi in (1, 3) else B4[:, 0, :, 0:w - 1]
                    kb.op("dve", lambda e, tmpv=tmpv, mx=mx, c=c, w=w: e.tensor_tensor(
                        out=mx[:, :, 0:w - 1], in0=tmpv, in1=HF[:, c, :, 15:15 + w - 1], op=ALU.subtract),
                        reads=[RA, RB_, rHF], writes=[RMX])

        def proj_mm(t, n, par):
            MX, RMX = MXs[par], RMXs[par]
            Yt, RYt = Yts[par], RYts[par]
            pss, rpss = PS[7 - par], RPS[7 - par]
            for c in range(DC):
                gi, no = c // 2, c % 2
                p, rp = next_ps(0, 6)
                for ci in range(2):
                    kb.op("pe", lambda e, p=p, gi=gi, no=no, ci=ci: e.matmul(
                        p[:, 0:n], lhsT=PW[:, 2 * gi + ci, no * 128:(no + 1) * 128], rhs=MX[:, 2 * gi + ci, 0:n],
                        start=(ci == 0), stop=(ci == 1)), reads=[RPW, RMX], writes=[rp])
                kb.op("act", lambda e, p=p, c=c: e.activation(out=Yt[:, c, 0:n], in_=p[:, 0:n], func=AF.Copy,
                                                              scale=C1[:, 96 + c:97 + c]), reads=[rp, RC], writes=[RYt])
                sq, rsq = sqs[nsq[0] % 2]
                nsq[0] += 1
                kb.op("act", lambda e, sq=sq, c=c: e.activation(out=sq[:, 0:n], in_=Yt[:, c, 0:n], func=AF.Square),
                      reads=[RYt], writes=[rsq])
                kb.op("pe", lambda e, sq=sq, c=c: e.matmul(pss[:, 0:n], lhsT=onesb[:, :], rhs=sq[:, 0:n],
                                                           start=(c == 0), stop=(c == DC - 1)), reads=[rsq, RC], writes=[rpss])

        def proj_post(t, n, par):
            post_tile(Yts[par], RYts[par], t, n, 3, l, PS[7 - par], RPS[7 - par], rposts[par][0], rposts[par][1])

        kb.op("dve", lambda e: e.memset(HFs[0][:, :, 0:15], 0.0), writes=[RHF[0]])
        def stage_a(t):
            HF, rHF = HFs[t % 2], RHF[t % 2]
            norm_stats(lambda c, t=t: X[:, c, ts(t)], RX[t], 512, rstd, rr, sqs)
            for c in range(DC):
                kb.op("dve", lambda e, c=c, t=t, HF=HF: e.scalar_tensor_tensor(
                    out=HF[:, c, 15:L], in0=X[:, c, ts(t)], scalar=g_col(l, 2, c), in1=rstd[:, 0:512],
                    op0=ALU.mult, op1=ALU.mult), reads=[RX[t], rr, RC], writes=[rHF])
            if t < 3:
                kb.op("act", lambda e, HF=HF, t=t: e.copy(out=HFs[(t + 1) % 2][:, :, 0:15], in_=HF[:, :, 512:L]),
                      reads=[rHF], writes=[RHF[(t + 1) % 2]])
            else:
                for half in range(2):
                    p, rp = next_ps(0, 6)
                    for j in range(4):
                        c = half * 4 + j
                        kb.op("pe", lambda e, p=p, c=c, j=j, HF=HF: e.transpose(out=p[0:15, j * 128:(j + 1) * 128], in_=HF[:, c, 512:L],
                                                                               identity=identf[:, :]), reads=[rHF, RC], writes=[rp])
                    kb.op("dve", lambda e, p=p, half=half: e.tensor_copy(out=stg[0:15, half * 512:(half + 1) * 512], in_=p[0:15, :]),
                          reads=[rp], writes=[Rstg])
                kb.dma("sp", o_pp[:, :], stg[0:15, :], reads=[Rstg])
            window_mix(HF[:, :, :].rearrange("p c (g l) -> p c g l", g=1), rHF, 1, L, t == 0, t % 2)

        stage_a(0)
        for t in range(4):
            proj_mm(t, 512, t % 2)
            if t < 3:
                stage_a(t + 1)
            proj_post(t, 512, t % 2)
        Ls = 19
        HS = HFs[0][:, :, 0:4 * Ls].rearrange("p c (g l) -> p c g l", g=4)
        rHS = RHF[0]
        sst, Rsst = stg, Rstg
        for b in range(4):
            kb.dma("sp", sst[:, :], state_pool[b], writes=[Rsst])
            for half in range(2):
                p, rp = next_ps(0, 6)
                for j in range(4):
                    c = half * 4 + j
                    kb.op("pe", lambda e, p=p, c=c, j=j: e.transpose(out=p[:, j * 32:j * 32 + 15], in_=sst[0:15, c * 128:(c + 1) * 128],
                                                                   identity=identf[0:15, 0:15]), reads=[Rsst, RC], writes=[rp])
                kb.op("dve", lambda e, p=p, half=half, b=b: e.tensor_copy(
                    out=HS[:, half * 4:half * 4 + 4, b, 0:15], in_=p[:, 0:128].rearrange("p (j n) -> p j n", j=4)[:, :, 0:15]),
                    reads=[rp], writes=[rHS])
            rcp = kb.res("cpsp%d" % b)
            kb.dma("sp", o_ps[b, 0:11, :], state_pool[b, 4:15, :], writes=[rcp])
        norm_stats(lambda c: X[:, c, SEQ:SEQ + 16], RX[4], 16, rstd, rr, sqs)
        for c in range(DC):
            kb.op("dve", lambda e, c=c: e.scalar_tensor_tensor(
                out=HS[:, c, :, 15:19], in0=X[:, c, SEQ:SEQ + 16].rearrange("p (b q) -> p b q", b=4), scalar=g_col(l, 2, c),
                in1=rstd[:, 0:16].rearrange("p (b q) -> p b q", b=4), op0=ALU.mult, op1=ALU.mult),
                reads=[RX[4], rr, RC], writes=[rHS])
        for b in range(4):
            for half in range(2):
                p, rp = next_ps(0, 6)
                for j in range(4):
                    c = half * 4 + j
                    kb.op("pe", lambda e, p=p, c=c, j=j, b=b: e.transpose(out=p[0:4, j * 128:(j + 1) * 128], in_=HS[:, c, b, 15:19],
                                                                         identity=identf[:, :]), reads=[rHS, RC], writes=[rp])
                kb.op("dve", lambda e, p=p, half=half: e.tensor_copy(out=stg[0:4, half * 512:(half + 1) * 512], in_=p[0:4, :]),
                      reads=[rp], writes=[Rstg])
            kb.dma("sp", o_ps[b, 11:15, :], stg[0:4, :], reads=[Rstg])
        window_mix(HS, rHS, 4, Ls, False, 0)
        proj_mm(4, 16, 0)
        proj_post(4, 16, 0)

    def phase(fn, *a):
        with ExitStack() as es:
            fn(*a, es)
            kb.barrier()

    setup_consts()
    phase(load_x)
    if stage >= 1:
        phase(ffn_seq, [(0, 0)])
    if stage >= 2:
        phase(mixer0)
    if stage >= 4:
        phase(ffn_seq, [(0, 1), (1, 0)])
    if stage >= 5:
        phase(mixer1)
    if stage >= 6:
        phase(ffn_seq, [(1, 1)])
    phase(store_x)
    kb.finish()
    return nc


_CACHE = {}


def _get_nc(stage, npool=NPOOLPG):
    if (stage, npool) not in _CACHE:
        _CACHE[(stage, npool)] = build(stage, npool)
    return _CACHE[(stage, npool)]


def kernel(x_prompt, x_sample, cache_k, cache_v, cache_logf, state_conv, state_pool, page_table,
           norm_g, ffn_w_gate, ffn_w_up, ffn_w_down, mix_w_in, fgate_b, conv_dw_w, conv_dw_b,
           conv_ln_g, conv_ln_b, mix_w_out, pool_w, pool_scale, _stage=None):
    stage = int(os.environ.get("KSTAGE", "99")) if _stage is None else _stage
    npool = int(np.asarray(cache_k).shape[1])
    ncores = int(os.environ.get("KCORES", str(NCORES)))
    nc = _get_nc(stage, npool)
    f = lambda a: np.ascontiguousarray(np.asarray(a))
    shared = {
        "cache_k": f(cache_k).reshape(npool * PAGE, DATT),
        "cache_v": f(cache_v).reshape(npool * PAGE, DATT),
        "cache_logf": f(cache_logf).reshape(npool * PAGE, NH),
        "norm_g": f(norm_g).reshape(12 * DC, 128),
        "ffn_w_gate": f(ffn_w_gate).reshape(4, D, DFF),
        "ffn_w_up": f(ffn_w_up).reshape(4, D, DFF),
        "ffn_w_down": f(ffn_w_down).reshape(4, DFF, D),
        "mix_w_in": f(mix_w_in).reshape(D, DIN),
        "fgate_b": f(fgate_b).reshape(NH),
        "conv_dw_w": f(conv_dw_w).reshape(CONVW * 4, 128),
        "conv_dw_b": f(conv_dw_b).reshape(4, 128),
        "conv_ln_g": f(conv_ln_g).reshape(4, 128),
        "conv_ln_b": f(conv_ln_b).reshape(4, 128),
        "mix_w_out": f(mix_w_out).reshape(D, D),
        "pool_w": f(pool_w).reshape(4, 256, 256),
        "pool_scale": f(pool_scale).reshape(DC, 128),
    }
    xp = f(x_prompt)
    xs = f(x_sample)
    sc = f(state_conv)
    spl = f(state_pool)
    pt = f(page_table)
    in_maps = []
    for c in range(ncores):
        m = dict(shared)
        m["x_prompt"] = xp[c]
        m["x_sample"] = xs[4 * c:4 * c + 4].reshape(NSAMP, D)
        m["state_conv"] = sc[0, 4 * c:4 * c + 4]
        m["state_pool"] = spl[0, 4 * c:4 * c + 4]
        m["page_table"] = pt[4 * c:4 * c + 4]
        in_maps.append(m)
    if os.environ.get("KTRACE"):
        res = run_bass_kernel_spmd(nc, in_maps, core_ids=list(range(ncores)), trace=True)
        print("KTRACE exec_time_ns", res.exec_time_ns)
    else:
        res = run_bass_kernel_spmd(nc, in_maps, core_ids=list(range(ncores)))
    R = list(res.results) + [res.results[0]] * (NCORES - ncores)
    cat = lambda k: np.stack([np.asarray(r[k]) for r in R])
    y_p = cat("y_prompt")
    y_s = cat("y_sample").reshape(32, 4, D)
    kp = cat("new_k_prompt").reshape(1, 8, SEQ, NH, HD)
    vp = cat("new_v_prompt").reshape(1, 8, SEQ, NH, HD)
    lp = cat("new_logf_prompt").reshape(1, 8, SEQ, NH)
    cp = cat("new_conv_prompt").reshape(1, 8, 30, DCONV)
    pp = cat("new_pool_prompt").reshape(1, 8, 15, D)
    ks = cat("new_k_sample").reshape(1, 32, 4, NH, HD)
    vs = cat("new_v_sample").reshape(1, 32, 4, NH, HD)
    ls = cat("new_logf_sample").reshape(1, 32, 4, NH)
    cs = cat("new_conv_sample").reshape(1, 32, 30, DCONV)
    ps = cat("new_pool_sample").reshape(1, 32, 15, D)
    return (y_p, y_s, kp, vp, lp, cp, pp, ks, vs, ls, cs, ps)
```
